# Optimizing a Trainium2 kernel written in Bass

```python
import jax, jax.numpy as jnp
from jax import lax
import numpy as np

D_MODEL = 1024
BATCH = 8
SEQ = 4096
DEPTH = 4

ATT_HEADS = 8
ATT_HEAD_DIM = 64
ATT_ROPE_DIM = ATT_HEAD_DIM // 4
ATT_ROPE_THETA = 500000.0
DILATED_GROUPS = ((128, 1), (512, 4), (2048, 16))
RET_HEADS = 8
RET_QK_DIM = 64
RET_V_DIM = 2 * RET_QK_DIM
RET_THETA = 10000.0
RET_CHUNK = 128
D_FF = 4 * D_MODEL
PLE_DIM = 256
NORM_EPS = 1e-6
GN_EPS = 1e-5
MASK_VALUE = -1e30

ATT_W = ATT_HEADS * ATT_HEAD_DIM
RET_QK_W = RET_HEADS * RET_QK_DIM
RET_V_W = RET_HEADS * RET_V_DIM
IN_SPLITS = (ATT_W, ATT_W, ATT_W, RET_QK_W, RET_QK_W, RET_V_W, RET_V_W, D_MODEL, D_MODEL)
IN_WIDTH = sum(IN_SPLITS)

kernel_name = "hybrid_dilated_attn_retention_block"


def rms_norm(x, gain=None):
    xf = x.astype(jnp.float32)
    y = xf * lax.rsqrt(jnp.mean(xf * xf, axis=-1, keepdims=True) + NORM_EPS)
    if gain is not None:
        y = y * gain.astype(jnp.float32)
    return y.astype(x.dtype)


def rotate(x, pos, rot_dim, theta):
    half = rot_dim // 2
    inv = theta ** (-jnp.arange(half, dtype=jnp.float32) * 2.0 / rot_dim)
    ang = pos.astype(jnp.float32)[:, None] * inv[None, :]
    cos = jnp.cos(ang)[:, None, :]
    sin = jnp.sin(ang)[:, None, :]
    xr = x[..., :rot_dim].astype(jnp.float32)
    x1, x2 = xr[..., :half], xr[..., half:]
    rot = jnp.concatenate([x1 * cos - x2 * sin, x2 * cos + x1 * sin], axis=-1).astype(x.dtype)
    return jnp.concatenate([rot, x[..., rot_dim:]], axis=-1)


def dilated_band_attention(q, k, v, dilation, half_width):
    b, s, h, dh = q.shape
    r = dilation
    L = s // r
    blk = half_width
    nb = -(-L // blk)
    Lp = nb * blk

    def to_sub(t):
        return t.reshape(b, L, r, h, dh).transpose(0, 2, 3, 1, 4)

    qs = jnp.pad(to_sub(q), ((0, 0), (0, 0), (0, 0), (0, Lp - L), (0, 0))).reshape(b, r, h, nb, blk, dh)

    def windows(t):
        t = jnp.pad(to_sub(t), ((0, 0), (0, 0), (0, 0), (blk, Lp - L + blk), (0, 0))).reshape(b, r, h, nb + 2, blk, dh)
        return jnp.concatenate([t[:, :, :, :-2], t[:, :, :, 1:-1], t[:, :, :, 2:]], axis=4)

    kw, vw = windows(k), windows(v)
    scores = jnp.einsum('brhnqd,brhnkd->brhnqk', qs.astype(jnp.float32), kw.astype(jnp.float32)) * (dh ** -0.5)
    qi = jnp.arange(blk)
    kj = jnp.arange(3 * blk)
    rel = qi[:, None] - kj[None, :] + blk
    key_idx = jnp.arange(nb)[:, None] * blk + kj[None, :] - blk
    valid = ((jnp.abs(rel) <= half_width)[None]
             & ((key_idx >= 0) & (key_idx < L))[:, None, :])
    scores = jnp.where(valid, scores, MASK_VALUE)
    lse = jax.nn.logsumexp(scores, axis=-1)
    probs = jnp.exp(scores - lse[..., None])
    out = jnp.einsum('brhnqk,brhnkd->brhnqd', probs, vw.astype(jnp.float32))
    out = out.reshape(b, r, h, Lp, dh)[:, :, :, :L].transpose(0, 3, 1, 2, 4).reshape(b, s, h, dh)
    lse = lse.reshape(b, r, h, Lp)[..., :L].transpose(0, 3, 1, 2).reshape(b, s, h)
    return out, lse


def dilated_attention_mixer(q, k, v):
    b, s, _ = q.shape
    pos = jnp.arange(s)
    q = rotate(q.reshape(b, s, ATT_HEADS, ATT_HEAD_DIM), pos, ATT_ROPE_DIM, ATT_ROPE_THETA)
    k = rotate(k.reshape(b, s, ATT_HEADS, ATT_HEAD_DIM), pos, ATT_ROPE_DIM, ATT_ROPE_THETA)
    v = v.reshape(b, s, ATT_HEADS, ATT_HEAD_DIM)
    outs, lses = [], []
    for window, dil in DILATED_GROUPS:
        o, l = dilated_band_attention(q, k, v, dil, window // (2 * dil))
        outs.append(o)
        lses.append(l)
    weights = jax.nn.softmax(jnp.stack(lses, axis=0), axis=0)
    out = jnp.sum(weights[..., None] * jnp.stack(outs, axis=0), axis=0)
    return out.reshape(b, s, ATT_W).astype(v.dtype)


def retention_direction(q, k, v, log_gamma, include_diag):
    b, h, s, dk = q.shape
    dv = v.shape[-1]
    c = RET_CHUNK
    n = s // c
    qc = q.reshape(b, h, n, c, dk)
    kc = k.reshape(b, h, n, c, dk)
    vc = v.reshape(b, h, n, c, dv)
    idx = jnp.arange(c, dtype=jnp.float32)
    diff = idx[:, None] - idx[None, :]
    inside = (diff >= 0) if include_diag else (diff > 0)
    lg = log_gamma[:, None, None]
    decay = jnp.where(inside[None], jnp.exp(jnp.maximum(diff, 0.0)[None] * lg), 0.0)
    intra = jnp.einsum('bhnid,bhnjd->bhnij', qc, kc) * decay[None, :, None]
    intra = jnp.einsum('bhnij,bhnje->bhnie', intra, vc)
    zeta = jnp.exp((c - 1 - idx)[None, :] * log_gamma[:, None])
    xi = jnp.exp((idx + 1)[None, :] * log_gamma[:, None])
    kv = jnp.einsum('bhnjd,bhnje->nbhde', kc * zeta[None, :, None, :, None], vc)
    chunk_decay = jnp.exp(c * log_gamma)[None, :, None, None]

    def step(state, kv_n):
        return chunk_decay * state + kv_n, state

    _, prev = lax.scan(step, jnp.zeros((b, h, dk, dv), jnp.float32), kv)
    cross = jnp.einsum('bhnid,nbhde->bhnie', qc * xi[None, :, None, :, None], prev)
    return (intra + cross).reshape(b, h, s, dv)


def retention_mixer(q, k, v, g, decay_logit):
    b, s, _ = q.shape
    pos = jnp.arange(s)
    q = rotate(q.reshape(b, s, RET_HEADS, RET_QK_DIM), pos, RET_QK_DIM, RET_THETA)
    k = rotate(k.reshape(b, s, RET_HEADS, RET_QK_DIM), pos, RET_QK_DIM, RET_THETA)
    q = q.astype(jnp.float32).transpose(0, 2, 1, 3)
    k = k.astype(jnp.float32).transpose(0, 2, 1, 3) * (RET_QK_DIM ** -0.5)
    v = v.reshape(b, s, RET_HEADS, RET_V_DIM).astype(jnp.float32).transpose(0, 2, 1, 3)
    log_gamma = jax.nn.log_sigmoid(decay_logit.astype(jnp.float32))
    fwd = retention_direction(q, k, v, log_gamma[0], True)
    bwd = retention_direction(jnp.flip(q, 2), jnp.flip(k, 2), jnp.flip(v, 2), log_gamma[1], False)
    y = fwd + jnp.flip(bwd, 2)
    mu = jnp.mean(y, axis=-1, keepdims=True)
    var = jnp.mean(jnp.square(y - mu), axis=-1, keepdims=True)
    y = (y - mu) * lax.rsqrt(var + GN_EPS)
    y = y.transpose(0, 2, 1, 3).reshape(b, s, RET_V_W)
    return (jax.nn.silu(g.astype(jnp.float32)) * y).astype(g.dtype)


def setup_inputs(seed: int = 0) -> dict:
    key = jax.random.key(seed)
    ks = jax.random.split(key, 16)

    def dense(k, shape, fan_in):
        return jax.random.normal(k, shape, jnp.float32) * (fan_in ** -0.5)

    def gain(k):
        return 1.0 + 0.05 * jax.random.normal(k, (DEPTH, D_MODEL), jnp.float32)

    expo = 5.0 + jnp.arange(RET_HEADS, dtype=jnp.float32)
    base_logit = jnp.log(jnp.exp2(expo) - 1.0)
    ret_decay_logit = base_logit[None, None, :] + 0.1 * jax.random.normal(ks[10], (DEPTH, 2, RET_HEADS), jnp.float32)
    return {
        "x": jax.random.normal(ks[0], (BATCH, SEQ, D_MODEL), jnp.float32),
        "p": jax.random.normal(ks[1], (DEPTH, BATCH, SEQ, PLE_DIM), jnp.float32),
        "w_in": dense(ks[2], (DEPTH, D_MODEL, IN_WIDTH), D_MODEL),
        "w_att_out": dense(ks[3], (DEPTH, ATT_W, D_MODEL), ATT_W),
        "w_ret_out": dense(ks[4], (DEPTH, RET_V_W, D_MODEL), RET_V_W),
        "w_out": dense(ks[5], (DEPTH, D_MODEL, D_MODEL), D_MODEL),
        "w_mlp_up": dense(ks[6], (DEPTH, D_MODEL, D_FF), D_MODEL),
        "w_mlp_down": dense(ks[7], (DEPTH, D_FF, D_MODEL), D_FF),
        "w_ple_gate": dense(ks[8], (DEPTH, D_MODEL, D_MODEL), D_MODEL),
        "w_ple_proj": dense(ks[9], (DEPTH, PLE_DIM, D_MODEL), PLE_DIM),
        "ret_decay_logit": ret_decay_logit,
        "norm_mix_pre": gain(ks[11]),
        "norm_mix_post": gain(ks[12]),
        "norm_mlp_pre": gain(ks[13]),
        "norm_mlp_post": gain(ks[14]),
        "norm_ple": gain(ks[15]),
    }


def reference(x, p, w_in, w_att_out, w_ret_out, w_out, w_mlp_up, w_mlp_down, w_ple_gate, w_ple_proj,
              ret_decay_logit, norm_mix_pre, norm_mix_post, norm_mlp_pre, norm_mlp_post, norm_ple):
    splits = np.cumsum(IN_SPLITS)[:-1].tolist()
    h = x
    for i in range(DEPTH):
        u = rms_norm(h, norm_mix_pre[i])
        proj = u @ w_in[i]
        qa, ka, va, qr, kr, vr, gr, gate_a, gate_b = jnp.split(proj, splits, axis=-1)
        att = dilated_attention_mixer(qa, ka, va)
        ret = retention_mixer(qr, kr, vr, gr, ret_decay_logit[i])
        merged = (jax.nn.sigmoid(gate_a) * (att @ w_att_out[i])
                  + jax.nn.sigmoid(gate_b) * (ret @ w_ret_out[i]))
        h = h + rms_norm(merged @ w_out[i], norm_mix_post[i])
        u = rms_norm(h, norm_mlp_pre[i])
        ff = jnp.square(jax.nn.relu(u @ w_mlp_up[i])) @ w_mlp_down[i]
        h = h + rms_norm(ff, norm_mlp_post[i])
        gate = jax.nn.sigmoid(rms_norm(h) @ w_ple_gate[i])
        h = h + gate * rms_norm(p[i] @ w_ple_proj[i], norm_ple[i])
    return h
```

```python
import numpy as np
import concourse.bass as bass
import concourse.mybir as mybir
from concourse.bass_utils import run_bass_kernel_spmd

F32 = mybir.dt.float32
BF16 = mybir.dt.bfloat16
ALU = mybir.AluOpType
AF = mybir.ActivationFunctionType


EPOCH = 30000


class Res:
    __slots__ = ("w", "r", "name", "excl")

    def __init__(self, name="", excl=False):
        self.w = None
        self.r = []
        self.name = name
        self.excl = excl


class Item:
    __slots__ = ("eng", "fn", "deps", "marked", "chan", "didx", "cpos", "val")

    def __init__(self, eng, fn):
        self.eng = eng
        self.fn = fn
        self.deps = []
        self.marked = False
        self.chan = None
        self.didx = 0
        self.cpos = 0
        self.val = 0


class Chan:
    __slots__ = ("n", "last", "sem", "id")

    def __init__(self, i):
        self.n = 0
        self.last = None
        self.sem = None
        self.id = i


class Sched:
    ENGS = ("pe", "act", "dve", "pool", "sp")

    def __init__(self):
        self.streams = {e: [] for e in self.ENGS}
        self.ncomp = {e: 0 for e in self.ENGS}
        self.chans = []

    def chan(self):
        c = Chan(len(self.chans))
        self.chans.append(c)
        return c

    def add(self, eng, fn, reads=(), writes=(), chan=None):
        it = Item(eng, fn)
        it.cpos = self.ncomp[eng]
        raw = []
        ex = [r for r in reads if r.excl]
        if ex:
            reads = [r for r in reads if not r.excl]
            writes = list(writes) + [r for r in ex if r not in writes]
        for r in reads:
            if r.w is not None:
                raw.append(r.w)
        for w in writes:
            if w.w is not None:
                raw.append(w.w)
            raw.extend(w.r)
        if chan is not None:
            it.chan = chan
            chan.n += 1
            it.didx = chan.n
            if chan.last is not None:
                raw.append(chan.last)
            chan.last = it
        seen = set()
        for d in raw:
            if id(d) in seen or d is it:
                continue
            seen.add(id(d))
            if d.chan is not None:
                it.deps.append(d)
                continue
            if d.eng == eng:
                if eng == "pe" and chan is None:
                    continue
                if chan is None and (self.ncomp[eng] - d.cpos) > 2:
                    continue
            d.marked = True
            it.deps.append(d)
        for r in reads:
            r.r.append(it)
        for w in writes:
            w.w = it
            w.r = []
        if chan is None:
            self.ncomp[eng] += 1
        self.streams[eng].append(it)
        return it

    def finalize(self):
        self.nmarked = {}
        for e in self.ENGS:
            v = 0
            for it in self.streams[e]:
                if it.chan is None and it.marked:
                    v += 1
                    it.val = v
            self.nmarked[e] = v
        return {e: max(1, -(-self.nmarked[e] // EPOCH)) for e in self.ENGS}

    def emit(self, nc, block, esems, csems):
        self.nwaits = 0
        self.log = {e: [] for e in self.ENGS}
        for c in self.chans:
            c.sem = csems[c.id]
        engobj = {"pe": block.tensor, "act": block.scalar, "dve": block.vector,
                  "pool": block.gpsimd, "sp": block.sync}

        def run(e):
            def body(eng):
                waited = {}
                for it in self.streams[e]:
                    need = {}
                    for d in it.deps:
                        if d.chan is not None:
                            key = ("c", d.chan.id)
                            sem = d.chan.sem
                            val = 16 * d.didx
                        else:
                            ep = (d.val - 1) // EPOCH
                            key = ("e", d.eng, ep)
                            sem = esems[d.eng][ep]
                            val = (d.val - 1) % EPOCH + 1
                        if waited.get(key, 0) < val and need.get(key, (None, 0))[1] < val:
                            need[key] = (sem, val)
                    for key, (sem, val) in need.items():
                        eng.wait_ge(sem, val)
                        waited[key] = val
                        self.nwaits += 1
                        self.log[e].append(f"   wait {key} >= {val}")
                    self.log[e].append(f"{'DMA' if it.chan is not None else 'op'} val={it.val if it.chan is None else ('c%d:%d' % (it.chan.id, 16*it.didx))} marked={it.marked} tag={getattr(it.fn, '_tag', '')}")
                    ins = it.fn(eng)
                    if it.chan is not None:
                        ins.then_inc(it.chan.sem, 16)
                    elif it.marked:
                        ins.then_inc(esems[e][(it.val - 1) // EPOCH], 1)
            engobj[e](body)

        for e in self.ENGS:
            if self.streams[e]:
                run(e)


from contextlib import ExitStack
import ml_dtypes
import os
DBG = os.environ.get("KDBG", "").split(",")

SEQ = 4096
DM = 1024
TB = 512
NTB = SEQ // TB
QA0, KA0, VA0 = 0, 512, 1024
QR0, KR0, VR0, GR0 = 1536, 2048, 2560, 3584
GA0, GB0 = 4608, 5632
EPS = 1e-6
GN_EPS = 1e-5
NRING = 3


def host_consts():
    pos = np.arange(SEQ, dtype=np.float32)
    inv_a = (np.float32(500000.0) ** (-(np.arange(8, dtype=np.float32) * 2.0 / 16.0))).astype(np.float32)
    ang_a = pos[None, :] * inv_a[:, None]
    CA = np.ones((128, SEQ), np.float32)
    SA = np.zeros((128, SEQ), np.float32)
    for hh in range(2):
        b = hh * 64
        CA[b:b + 8] = np.cos(ang_a)
        CA[b + 8:b + 16] = np.cos(ang_a)
        SA[b:b + 8] = -np.sin(ang_a)
        SA[b + 8:b + 16] = np.sin(ang_a)
    inv_r = (np.float32(10000.0) ** (-(np.arange(32, dtype=np.float32) * 2.0 / 64.0))).astype(np.float32)
    ang_r = pos[None, :] * inv_r[:, None]
    CR = np.zeros((128, SEQ), np.float32)
    SR = np.zeros((128, SEQ), np.float32)
    for hh in range(2):
        b = hh * 64
        CR[b:b + 32] = np.cos(ang_r)
        CR[b + 32:b + 64] = np.cos(ang_r)
        SR[b:b + 32] = -np.sin(ang_r)
        SR[b + 32:b + 64] = np.sin(ang_r)
    rope = np.stack([CA, SA, CR, SR]).astype(np.float32)
    permA = np.zeros((128, 128), np.float32)
    permR = np.zeros((128, 128), np.float32)
    for hh in range(2):
        b = hh * 64
        for d in range(8):
            permA[b + d + 8, b + d] = 1.0
            permA[b + d, b + d + 8] = 1.0
        for d in range(32):
            permR[b + d + 32, b + d] = 1.0
            permR[b + d, b + d + 32] = 1.0
    pp = np.arange(128)[:, None]
    xx = np.arange(256)[None, :]
    maskb = np.where((xx >= pp) & (xx <= pp + 128), 0.0, -30000.0).astype(np.float32)
    jj = np.arange(128)[:, None].astype(np.float32)
    ii = np.arange(128)[None, :].astype(np.float32)
    BIG = 2.0e6
    distF = np.where(ii >= jj, ii - jj, BIG).astype(np.float32)
    distB = np.where(jj > ii, jj - ii, BIG).astype(np.float32)
    zexp = np.zeros((128, 4), np.float32)
    zexp[:, 0] = 127.0 - np.arange(128)
    zexp[:, 1] = np.arange(128)
    zexp[:, 2] = 128.0
    xexp = np.zeros((128, 2, 128), np.float32)
    xexp[:, 0, :] = np.arange(128)[None, :] + 1.0
    xexp[:, 1, :] = 128.0 - np.arange(128)[None, :]
    misc = np.concatenate([np.eye(128, dtype=np.float32), permA, permR, maskb, distF, distB,
                           zexp, xexp.reshape(128, 256)], axis=1)
    return rope, np.ascontiguousarray(misc)


MISC_W = 128 * 3 + 256 + 128 * 2 + 4 + 256


class _Stop(Exception):
    pass


def build_program(NL, stop=None):
    nc = bass.Bass("TRN2", target_bir_lowering=False)

    def din(name, shape, dt=F32):
        return nc.dram_tensor(name, shape, dt, kind="ExternalInput").ap()

    x_d = din("x", [SEQ, DM])
    p_d = din("p", [NL, SEQ, 256])
    w_in_d = din("w_in", [NL, DM, 6656])
    w_ao_d = din("w_att_out", [NL, 512, DM])
    w_ro_d = din("w_ret_out", [NL, 1024, DM])
    w_out_d = din("w_out", [NL, DM, DM])
    w_up_d = din("w_mlp_up", [NL, DM, 4096])
    w_dn_d = din("w_mlp_down", [NL, 4096, DM])
    w_pg_d = din("w_ple_gate", [NL, DM, DM])
    w_pp_d = din("w_ple_proj", [NL, 256, DM])
    dec_d = din("dec", [NL, 16])
    gains_d = din("gains", [128, NL * 5 * 8])
    rope_d = din("rope", [4, 128, SEQ])
    misc_d = din("misc", [128, MISC_W])
    out_d = nc.dram_tensor("out", [SEQ, DM], F32, kind="ExternalOutput").ap()
    hT_d = nc.dram_tensor("hT_scr", [DM, SEQ], F32).ap()
    _dk = dict(kind="ExternalOutput") if "dump" in DBG else {}
    attS_d = nc.dram_tensor("att_scr", [8, 64, SEQ], BF16, **_dk).ap()
    retS_d = nc.dram_tensor("ret_scr", [8, 128, SEQ], BF16, **_dk).ap()
    hT_v = hT_d.rearrange("(kc p) t -> p kc t", p=128)
    R_hT = [Res(f"hT{i}") for i in range(NTB)]
    R_attS = [Res(f"attS{i}") for i in range(NTB)]
    R_retS = [Res(f"retS{i}") for i in range(NTB)]

    es = ExitStack()
    with es:
        def sb(name, shp, dt):
            return es.enter_context(nc.sbuf_tensor("sb_" + name, shp, dt))

        def pst(name, shp, dt):
            return es.enter_context(nc.psum_tensor("pp_" + name, shp, dt))

        S = Sched()
        uT = sb("uT", [128, 8, SEQ], BF16)
        R_uT = [Res(f"uT{i}") for i in range(NTB)]
        misc32 = sb("misc32", [128, MISC_W], F32)
        miscbf = sb("miscbf", [128, 640], BF16)
        R_misc = Res("misc")
        gains = sb("gains", [128, NL * 5 * 8], F32)
        onesbf = sb("onesbf", [128, 128], BF16)
        onesgn = sb("onesgn", [128, 128], BF16)
        ones32 = sb("ones32", [128, 64], F32)
        R_const = Res("const")
        ident32 = misc32[:, 0:128]
        o_ = 384 + 256
        distF = misc32[:, o_:o_ + 128]
        distB = misc32[:, o_ + 128:o_ + 256]
        zexp = misc32[:, o_ + 256:o_ + 260]
        xexp = misc32[:, o_ + 260:o_ + 516].rearrange("p (k i) -> p k i", k=2)
        identbf = miscbf[:, 0:128]
        permAbf = miscbf[:, 128:256]
        permRbf = miscbf[:, 256:384]
        maskbf = miscbf[:, 384:640]

        ring = [sb(f"ring{i}", [128, 4096], BF16) for i in range(NRING)]
        R_ring = [Res(f"ring{i}") for i in range(NRING)]
        C_ring = [S.chan() for _ in range(NRING)]

        A1 = sb("A1", [128, 4096], BF16); R_A1 = Res("A1")
        A2 = sb("A2", [128, 4096], BF16); R_A2 = Res("A2")
        A3 = sb("A3", [128, 4096], BF16); R_A3 = Res("A3")
        A4 = sb("A4", [128, 4096], BF16); R_A4 = Res("A4")
        A5 = sb("A5", [128, 4160], BF16); R_A5 = Res("A5")
        A6 = sb("A6", [128, 4160], BF16); R_A6 = Res("A6")
        A7 = sb("A7", [128, 4096], F32); R_A7 = Res("A7")
        A8 = sb("A8", [128, 4096], F32); R_A8 = Res("A8")
        NT32 = 6
        t32 = [sb(f"t32_{i}", [128, 512], F32) for i in range(NT32)]
        R_t32 = [Res(f"t32_{i}") for i in range(NT32)]
        NTB16 = 6
        t16 = [sb(f"t16_{i}", [128, 512], BF16) for i in range(NTB16)]
        R_t16 = [Res(f"t16_{i}") for i in range(NTB16)]
        ropeC = sb("ropeC", [128, 512], F32); ropeS = sb("ropeS", [128, 512], F32)
        R_rope = Res("rope")
        C_rope = [S.chan(), S.chan()]
        dect = sb("dect", [128, 1, 16], F32)
        lg = sb("lg", [128, 16], F32)
        rtab = sb("rtab", [128, 2 * 128 + 4 * 128 + 8], F32)
        R_dec = Res("dec"); R_rtab = Res("rtab")
        st32 = [sb(f"st32_{i}", [128, 128], F32) for i in range(2)]
        R_st32 = [Res("st32a"), Res("st32b")]

        rsb_ = [sb(f"rs_{i}", [128, 512], F32) for i in range(2)]
        R_rsb = [Res("rs0"), Res("rs1")]
        cnt = {"t32": 0, "t16": 0, "ps": 0, "rs": 0}
        if "verbose" in DBG:
            print("SBUF bytes remaining after alloc:", nc.sbuf_bytes_remaining)

        def T32():
            i = cnt["t32"] % NT32; cnt["t32"] += 1
            return t32[i], R_t32[i]

        def T16():
            i = cnt["t16"] % NTB16; cnt["t16"] += 1
            return t16[i], R_t16[i]

        psb = [pst(f"ps{i}", [128, 512], F32) for i in range(8)]
        R_ps = [Res(f"ps{i}", excl=True) for i in range(8)]

        def PS():
            i = cnt["ps"] % 6; cnt["ps"] += 1
            return psb[i], R_ps[i]

        def MM(out, lhsT, rhs, start, stop, rd, wr):
            return S.add("pe", lambda e: e.matmul(out, lhsT, rhs, start=start, stop=stop), reads=rd, writes=wr)

        def TR(out, in_, ident, rd, wr):
            return S.add("pe", lambda e: e.transpose(out, in_, ident), reads=rd, writes=wr)

        def ACT(out, in_, func, rd, wr, scale=1.0, bias=0.0):
            return S.add("act", lambda e: e.activation(out, in_, func, bias=bias, scale=scale), reads=rd, writes=wr)

        def TT(out, a, b, op, rd, wr, eng="dve"):
            return S.add(eng, lambda e: e.tensor_tensor(out, a, b, op), reads=rd, writes=wr)

        def STT(out, in0, scalar, in1, op0, op1, rd, wr, eng="dve"):
            return S.add(eng, lambda e: e.scalar_tensor_tensor(out, in0, scalar, in1, op0, op1), reads=rd, writes=wr)

        def TS(out, in0, s1, s2, op0, op1, rd, wr, eng="dve"):
            if s2 is None:
                return S.add(eng, lambda e: e.tensor_scalar(out, in0, s1, None, op0), reads=rd, writes=wr)
            return S.add(eng, lambda e: e.tensor_scalar(out, in0, s1, s2, op0, op1), reads=rd, writes=wr)

        def CP(out, in_, rd, wr, eng="dve"):
            return S.add(eng, lambda e: e.tensor_copy(out, in_), reads=rd, writes=wr)

        def RECIP(out, in_, rd, wr):
            return S.add("dve", lambda e: e.reciprocal(out, in_), reads=rd, writes=wr)

        def MEMSET(ap, val, wr, eng="dve"):
            return S.add(eng, lambda e: e.memset(ap, val), writes=wr)

        def DMA(q, out, in_, chan, rd, wr):
            return S.add(q, lambda e: e.dma_start(out=out, in_=in_), reads=rd, writes=wr, chan=chan)

        out_dmas = []

        def ss(start, count, step):
            return slice(start, start + step * (count - 1) + 1, step)

        wq = {"specs": [], "issued": 0, "used": 0, "base": 0}

        def w_issue_upto(k):
            while wq["issued"] < min(k, len(wq["specs"])):
                i = wq["issued"]
                b = i % NRING
                for (dst_fn, src) in wq["specs"][i]:
                    DMA("pool", dst_fn(ring[b]), src, C_ring[b], [], [R_ring[b]])
                wq["issued"] += 1

        def WNEXT(keep=False):
            i = wq["used"]
            if not keep:
                wq["base"] = i
            w_issue_upto(wq["base"] + NRING)
            wq["used"] += 1
            b = i % NRING
            return ring[b], R_ring[b]

        def panel(src_ap, kcn, ncols, col0=0, width=None, parts=128):
            width = width or ncols

            def dst(rb):
                return rb[0:parts, 0:kcn * width].rearrange("p (k f) -> p k f", f=width)[:, :, col0:col0 + ncols]
            return (dst, src_ap)

        def attn_specs(l, hp):
            wv = w_in_d[l].rearrange("(kc p) f -> p kc f", p=128)
            return [[panel(wv[:, :, QA0 + hp * 128:QA0 + hp * 128 + 128], 8, 128, 0, 384),
                     panel(wv[:, :, KA0 + hp * 128:KA0 + hp * 128 + 128], 8, 128, 128, 384),
                     panel(wv[:, :, VA0 + hp * 128:VA0 + hp * 128 + 128], 8, 128, 256, 384)]]

        def ret_specs(l, hp):
            wv = w_in_d[l].rearrange("(kc p) f -> p kc f", p=128)
            return [[panel(wv[:, :, QR0 + hp * 128:QR0 + hp * 128 + 128], 8, 128, 0, 512),
                     panel(wv[:, :, KR0 + hp * 128:KR0 + hp * 128 + 128], 8, 128, 128, 512),
                     panel(wv[:, :, VR0 + hp * 256:VR0 + hp * 256 + 256], 8, 256, 256, 512)],
                    [panel(wv[:, :, GR0 + hp * 256:GR0 + hp * 256 + 256], 8, 256, 0, 256)]]

        def phc_specs(l):
            wv = w_in_d[l].rearrange("(kc p) f -> p kc f", p=128)
            sp = []
            wao = w_ao_d[l].rearrange("(h p) f -> p h f", p=64)
            wro = w_ro_d[l].rearrange("(kc p) f -> p kc f", p=128)
            for fc in range(8):
                c0 = fc * 128
                sp.append([panel(wv[:, :, GA0 + c0:GA0 + c0 + 128], 8, 128, 0, 512),
                           panel(wv[:, :, GB0 + c0:GB0 + c0 + 128], 8, 128, 128, 512),
                           panel(wao[:, :, c0:c0 + 128], 8, 128, 256, 512, parts=64),
                           panel(wro[:, :, c0:c0 + 128], 8, 128, 384, 512)])
            wo = w_out_d[l].rearrange("(kc p) f -> p kc f", p=128)
            for half in range(2):
                sp.append([panel(wo[:, :, half * 512:half * 512 + 512], 8, 512)])
            wu = w_up_d[l].rearrange("(kc p) f -> p kc f", p=128)
            wd = w_dn_d[l].rearrange("(kc p) f -> p kc f", p=128)
            for half in range(2):
                for q4 in range(4):
                    c0 = half * 2048 + q4 * 512
                    sp.append([panel(wu[:, :, c0:c0 + 512], 8, 512)])
                for q4 in range(4):
                    sp.append([panel(wd[:, half * 16:half * 16 + 16, q4 * 256:q4 * 256 + 256], 16, 256)])
            wpp = w_pp_d[l].rearrange("(kc p) f -> p kc f", p=128)
            sp.append([panel(wpp[:, :, :], 2, 1024)])
            wpg = w_pg_d[l].rearrange("(kc p) f -> p kc f", p=128)
            for half in range(2):
                sp.append([panel(wpg[:, :, half * 512:half * 512 + 512], 8, 512)])
            return sp

        for l in range(NL):
            for hp in range(4):
                wq["specs"] += attn_specs(l, hp)
            for hp in range(4):
                wq["specs"] += ret_specs(l, hp)
            for tb in range(NTB):
                wq["specs"] += phc_specs(l)

        C_c = [S.chan() for _ in range(4)]
        DMA("sp", misc32[:], misc_d, C_c[0], [], [R_misc])
        DMA("sp", gains[:], gains_d, C_c[1], [], [R_const])
        DMA("pool", miscbf[:], misc_d[:, 0:640], C_c[2], [], [R_misc])
        MEMSET(onesbf[:], 1.0, [R_const])
        MEMSET(onesgn[:], 1.0 / 128.0, [R_const])
        MEMSET(ones32[:], 1.0, [R_const])

        def gcol(l, which, kc):
            c = (l * 5 + which) * 8 + kc
            return gains[:, c:c + 1]

        def rms_rstd(src_fn, src_res, n_feat_chunks=8, inv_n=1.0 / 1024.0):
            ps, rps = PS()
            for kc in range(n_feat_chunks):
                sq, rsq = T16()
                ACT(sq[:], src_fn(kc), AF.Square, src_res, [rsq])
                MM(ps[:], onesbf[:], sq[:], kc == 0, kc == n_feat_chunks - 1, [rsq, R_const], [rps])
            i_ = cnt["rs"] % 2; cnt["rs"] += 1
            rs, rrs = rsb_[i_], R_rsb[i_]
            ACT(rs[:], ps[:], AF.Sqrt, [rps], [rrs], scale=inv_n, bias=EPS)
            RECIP(rs[:], rs[:], [], [rrs])
            return rs, rrs

        def prenorm(hblk_fn, hres, l, which, dst_fn, dres):
            rs, rrs = rms_rstd(hblk_fn, hres)
            for kc in range(8):
                if which is None:
                    TT(dst_fn(kc), hblk_fn(kc), rs[:], ALU.mult, hres + [rrs], dres)
                else:
                    STT(dst_fn(kc), hblk_fn(kc), gcol(l, which, kc), rs[:], ALU.mult, ALU.mult,
                        hres + [rrs, R_const], dres)

        def postnorm_add(z_fn, zres, l, which, h_fn, hres):
            rs, rrs = rms_rstd(z_fn, zres)
            for kc in range(8):
                tmp, rt = T32()
                STT(tmp[:], z_fn(kc), gcol(l, which, kc), rs[:], ALU.mult, ALU.mult, zres + [rrs, R_const], [rt])
                TT(h_fn(kc), h_fn(kc), tmp[:], ALU.add, [rt], hres)

        hblk = A7[:, :].rearrange("p (k t) -> p k t", t=512)
        zblk = A8[:, :].rearrange("p (k t) -> p k t", t=512)

        def hfn(kc):
            return hblk[:, kc, :]

        def zfn(kc):
            return zblk[:, kc, :]

        C_h = [S.chan(), S.chan()]
        C_misc = [S.chan() for _ in range(6)]
        C_out = [S.chan() for _ in range(4)]

        try:
            xblk = A8[:, :].rearrange("p (t f) -> p t f", f=1024)
            for tb in range(NTB):
                cols = slice(tb * TB, (tb + 1) * TB)
                DMA("sp", xblk, x_d[cols, :].rearrange("(t p) f -> p t f", p=128), C_h[0], [], [R_A8])
                for kc in range(8):
                    ps, rps = PS()
                    for tt in range(4):
                        TR(ps[:, tt * 128:(tt + 1) * 128], xblk[:, tt, kc * 128:(kc + 1) * 128], ident32,
                           [R_A8, R_misc], [rps])
                    ACT(hblk[:, kc, :], ps[:], AF.Copy, [rps], [R_A7])
                DMA("sp", hT_v[:, :, cols], hblk, C_h[1], [R_A7], [R_hT[tb]])
                prenorm(hfn, [R_A7], 0, 0, lambda kc: uT[:, kc, cols], [R_uT[tb]])

            if stop == "p0":
                raise _Stop()
            Qrot, Krot, VT = A1, A2, A3
            for l in range(NL):
                for hp in range(4):
                    wb, rwb = WNEXT()
                    if "noattn" in DBG:
                        continue
                    wv = wb[:, 0:8 * 384].rearrange("p (k f) -> p k f", f=384)
                    for tb in range(NTB):
                        cols = slice(tb * TB, (tb + 1) * TB)
                        DMA("sp", ropeC[:], rope_d[0][:, cols], C_rope[0], [], [R_rope])
                        DMA("sp", ropeS[:], rope_d[1][:, cols], C_rope[1], [], [R_rope])
                        for wi, (dstT, rdst) in enumerate(((Qrot, R_A1), (Krot, R_A2))):
                            ps, rps = PS()
                            for kc in range(8):
                                MM(ps[:], wv[:, kc, wi * 128:(wi + 1) * 128], uT[:, kc, cols], kc == 0, kc == 7,
                                   [rwb, R_uT[tb]], [rps])
                            qraw, rq = T16()
                            ACT(qraw[:], ps[:], AF.Copy, [rps], [rq])
                            ps2, rps2 = PS()
                            MM(ps2[:], permAbf, qraw[:], True, True, [rq, R_misc], [rps2])
                            t1, rt1 = T32()
                            TT(t1[:], ps[:], ropeC[:], ALU.mult, [rps, R_rope], [rt1])
                            t2, rt2 = T32()
                            TT(t2[:], ps2[:], ropeS[:], ALU.mult, [rps2, R_rope], [rt2])
                            TT(dstT[:, cols], t1[:], t2[:], ALU.add, [rt1, rt2], [rdst])
                        ps, rps = PS()
                        for kc in range(8):
                            MM(ps[:], wv[:, kc, 256:384], uT[:, kc, cols], kc == 0, kc == 7, [rwb, R_uT[tb]], [rps])
                        ACT(VT[:, cols], ps[:], AF.Copy, [rps], [R_A3])
                    accs = ((A7, R_A7), (A8, R_A8))
                    for g, r in enumerate((1, 4, 16)):
                        for hh in range(2):
                            head = hp * 2 + hh
                            rows = slice(hh * 64, hh * 64 + 64)
                            acc, R_acc = accs[hh]
                            Vb, RVb = (A5, R_A5) if g % 2 == 0 else (A6, R_A6)
                            Vaug = Vb[:, 0:4160].rearrange("p (c h d) -> p c h d", h=2, d=65)
                            J = 32 // r
                            L = SEQ // r
                            if hh == 0:
                                MEMSET(Vaug[:, :, :, 64:65], 1.0, [RVb])
                                for j0 in range(0, 32, 4):
                                    ps, rps = PS()
                                    psv = ps[:, 0:256].bitcast(BF16)
                                    for jq in range(4):
                                        j = j0 + jq
                                        c, jj = divmod(j, J)
                                        t0 = c + r * 128 * jj
                                        TR(psv[:, jq * 128:(jq + 1) * 128], VT[:, ss(t0, 128, r)], identbf,
                                           [R_A3, R_misc], [rps])
                                    ACT(Vaug[:, j0:j0 + 4, :, 0:64],
                                        psv.rearrange("p (c h d) -> p c h d", h=2, d=64), AF.Copy, [rps], [RVb])
                            for c in range(r):
                                for jj in range(J):
                                    j = c * J + jj
                                    x0 = 64 if jj == 0 else 0
                                    x1 = 192 if jj == J - 1 else 256
                                    n = x1 - x0
                                    l0 = 128 * jj - 64 + x0
                                    kt0 = c + r * 128 * jj
                                    ktok = ss(kt0, 128, r)
                                    qt0 = c + r * l0
                                    qtok = ss(qt0, n, r)
                                    st, rst = PS()
                                    MM(st[:, 0:n], Krot[rows, ktok], Qrot[rows, qtok], True, False, [R_A1, R_A2], [rst])
                                    MM(st[:, 0:n], identbf, maskbf[:, x0:x1], False, True, [R_misc], [rst])
                                    PT, rpt = T16()
                                    ACT(PT[:, 0:n], st[:, 0:n], AF.Exp, [rst], [rpt], scale=0.125)
                                    na = 128 - x0
                                    for (m, pc0, pc1) in ((jj, 0, na), (jj + 1, na, n)):
                                        if pc1 <= pc0:
                                            continue
                                        b = m // 4
                                        ql0 = 128 * m - 64 + (x0 if m == jj else 0)
                                        oc0 = ql0 + 64 - 512 * b
                                        Ob, ROb = psb[6 + b % 2], R_ps[6 + b % 2]
                                        if m == jj:
                                            stt, stp = (jj == 0), True
                                        else:
                                            stt, stp = True, (jj == J - 1)
                                        MM(Ob[0:65, oc0:oc0 + (pc1 - pc0)], Vaug[:, j, hh, :], PT[:, pc0:pc1], stt, stp,
                                           [RVb, rpt], [ROb])
                                    evs = []
                                    if jj % 4 == 3:
                                        evs.append(jj // 4)
                                    if jj == J - 1:
                                        if J % 4 != 0:
                                            evs.append((J - 1) // 4)
                                        else:
                                            evs.append(J // 4)
                                    for b in evs:
                                        lo = max(0, 512 * b - 64)
                                        hi = min(L, 512 * b + 448)
                                        if hi <= lo:
                                            continue
                                        Ob, ROb = psb[6 + b % 2], R_ps[6 + b % 2]
                                        oc = slice(lo + 64 - 512 * b, hi + 64 - 512 * b)
                                        tk = ss(c + r * lo, hi - lo, r)
                                        if g == 0:
                                            ACT(acc[0:65, tk], Ob[0:65, oc], AF.Copy, [ROb], [R_acc])
                                        else:
                                            TT(acc[0:65, tk], Ob[0:65, oc], acc[0:65, tk], ALU.add, [ROb], [R_acc])
                    for hh in range(2):
                        head = hp * 2 + hh
                        acc, R_acc = accs[hh]
                        for tb in range(NTB):
                            cols = slice(tb * TB, (tb + 1) * TB)
                            RECIP(acc[64:65, cols], acc[64:65, cols], [], [R_acc])
                            ps, rps = PS()
                            MM(ps[0:64, :], ones32[64:65, 0:64], acc[64:65, cols], True, True, [R_acc, R_const], [rps])
                            ot, rot_ = T16()
                            TT(ot[0:64, :], acc[0:64, cols], ps[0:64, :], ALU.mult, [R_acc, rps], [rot_])
                            DMA("sp", attS_d[head][:, cols], ot[0:64, :], C_misc[tb % 2], [rot_], [R_attS[tb]])

                if stop == "attn":
                    raise _Stop()
                DMA("sp", dect[:], dec_d[l:l + 1, :].partition_broadcast(128), C_misc[2], [], [R_dec])
                ACT(lg[:], dect[:, 0, :], AF.Exp, [R_dec], [R_dec], scale=-1.0)
                ACT(lg[:], lg[:], AF.Ln, [], [R_dec], bias=1.0)
                TS(lg[:], lg[:], -1.0, None, ALU.mult, None, [], [R_dec])
                DT = rtab[:, 0:256].rearrange("p (h i) -> p h i", h=2)
                XI = rtab[:, 256:768].rearrange("p (h d i) -> p h d i", h=2, d=2)
                ZE = rtab[:, 768:772]
                CD = rtab[:, 772:776]
                for hp in range(4):
                    wb, rwb = WNEXT()
                    wv = wb[:, 0:8 * 512].rearrange("p (k f) -> p k f", f=512)
                    wgb, rwgb = WNEXT(keep=True)
                    wg = wgb[:, 0:8 * 256].rearrange("p (k f) -> p k f", f=256)
                    for hh in range(2):
                        head = hp * 2 + hh
                        lf = lg[:, head:head + 1]
                        lb = lg[:, 8 + head:9 + head]
                        ef, ref = T32()
                        ACT(ef[:, 0:128], distF, AF.Exp, [R_misc, R_dec], [ref], scale=lf)
                        ACT(ef[:, 128:256], distB, AF.Exp, [R_misc, R_dec], [ref], scale=lb)
                        TT(DT[:, hh, :], ef[:, 0:128], ef[:, 128:256], ALU.add, [ref], [R_rtab])
                        TS(DT[:, hh, :], DT[:, hh, :], 0.125, None, ALU.mult, None, [], [R_rtab])
                        ACT(XI[:, hh, 0, :], xexp[:, 0, :], AF.Exp, [R_misc, R_dec], [R_rtab], scale=lf)
                        ACT(XI[:, hh, 1, :], xexp[:, 1, :], AF.Exp, [R_misc, R_dec], [R_rtab], scale=lb)
                        ACT(ZE[:, hh * 2:hh * 2 + 1], zexp[:, 0:1], AF.Exp, [R_misc, R_dec], [R_rtab], scale=lf)
                        ACT(ZE[:, hh * 2 + 1:hh * 2 + 2], zexp[:, 1:2], AF.Exp, [R_misc, R_dec], [R_rtab], scale=lb)
                        ACT(CD[:, hh * 2:hh * 2 + 1], zexp[:, 2:3], AF.Exp, [R_misc, R_dec], [R_rtab], scale=lf)
                        ACT(CD[:, hh * 2 + 1:hh * 2 + 2], zexp[:, 2:3], AF.Exp, [R_misc, R_dec], [R_rtab], scale=lb)
                    TS(XI[:, :, :, :], XI[:, :, :, :], 0.125, None, ALU.mult, None, [], [R_rtab])
                    if stop == "ret1":
                        raise _Stop()
                    for tb in range(NTB):
                        cols = slice(tb * TB, (tb + 1) * TB)
                        if "norope" not in DBG:
                            DMA("sp", ropeC[:], rope_d[2][:, cols], C_rope[0], [], [R_rope])
                            DMA("sp", ropeS[:], rope_d[3][:, cols], C_rope[1], [], [R_rope])
                        if "ret1b" in DBG:
                            raise _Stop()
                        for wi, (dstT, rdst) in (((1, (Krot, R_A2)), (0, (Qrot, R_A1))) if 'kfirst' in DBG else enumerate(((Qrot, R_A1), (Krot, R_A2)))):
                            if "samebuf" in DBG:
                                if wi == 0:
                                    _sv = dict(cnt)
                                else:
                                    cnt.update(_sv)
                            ps, rps = PS()
                            for kc in range(8):
                                MM(ps[:], wv[:, kc, wi * 128:(wi + 1) * 128], uT[:, kc, cols], kc == 0, kc == 7,
                                   [rwb, R_uT[tb]], [rps])
                            if "x1" in DBG and wi == 1:
                                raise _Stop()
                            qraw, rq = T16()
                            ACT(qraw[:], ps[:], AF.Copy, [rps], [rq])
                            if "x2" in DBG and wi == 1:
                                raise _Stop()
                            ps2, rps2 = PS()
                            MM(ps2[:], permRbf, qraw[:], True, True, [rq, R_misc], [rps2])
                            if "x3" in DBG and wi == 1:
                                raise _Stop()
                            t1, rt1 = T32()
                            TT(t1[:], ps[:], ropeC[:], ALU.mult, [rps, R_rope] + ([rq] if "serial" in DBG else []), [rt1])
                            if "x4" in DBG and wi == 1:
                                raise _Stop()
                            t2, rt2 = T32()
                            TT(t2[:], ps2[:], ropeS[:], ALU.mult, [rps2, R_rope], [rt2])
                            if "x5" in DBG and wi == 1:
                                raise _Stop()
                            TT(dstT[:, cols], t1[:], t2[:], ALU.add, [rt1, rt2], [rdst])
                            if "ret2a" in DBG:
                                raise _Stop()
                        if "ret2b" in DBG:
                            raise _Stop()
                    if stop == "ret2":
                        raise _Stop()
                    Vtm = A8[:, :].bitcast(BF16).rearrange("p (c e) -> p c e", e=256)
                    for ch0 in range(0, 32, 2):
                        ps, rps = PS()
                        for cq in range(2):
                            ch = ch0 + cq
                            for kc in range(8):
                                MM(ps[:, cq * 256:(cq + 1) * 256], uT[:, kc, ch * 128:(ch + 1) * 128], wv[:, kc, 256:512],
                                   kc == 0, kc == 7, [rwb, R_uT[ch // 4]], [rps])
                        ACT(Vtm[:, ch0:ch0 + 2, :], ps[:].rearrange("p (c e) -> p c e", e=256), AF.Copy, [rps], [R_A8])
                    if stop == "ret3":
                        raise _Stop()
                    Kz = [A5[:, 0:4096].rearrange("p (c d) -> p c d", d=128), A6[:, 0:4096].rearrange("p (c d) -> p c d", d=128)]
                    RKz = [R_A5, R_A6]
                    for ch0 in range(0, 32, 4):
                        ps, rps = PS()
                        psv = ps[:, 0:256].bitcast(BF16)
                        for cq in range(4):
                            ch = ch0 + cq
                            TR(psv[:, cq * 128:(cq + 1) * 128], Krot[:, ch * 128:(ch + 1) * 128], identbf,
                               [R_A2, R_misc], [rps])
                        pv = psv.rearrange("p (c d) -> p c d", d=128)
                        for hh in range(2):
                            for di in range(2):
                                ACT(Kz[di][:, ch0:ch0 + 4, hh * 64:(hh + 1) * 64], pv[:, :, hh * 64:(hh + 1) * 64], AF.Copy,
                                    [rps, R_rtab], [RKz[di]], scale=ZE[:, hh * 2 + di:hh * 2 + di + 1])
                    if stop == "ret4":
                        raise _Stop()
                    for hh in range(2):
                        head = hp * 2 + hh
                        rows = slice(hh * 64, hh * 64 + 64)
                        Sst = A7[:, :].bitcast(BF16).rearrange("p (d c e) -> p d c e", d=2, e=128)
                        for di in range(2):
                            order = list(range(0, 31)) if di == 0 else list(range(31, 0, -1))
                            cur = None
                            for k4 in range(0, len(order), 4):
                                grp = order[k4:k4 + 4]
                                ps, rps = PS()
                                for qi, n_ in enumerate(grp):
                                    MM(ps[:, qi * 128:(qi + 1) * 128], Kz[di][:, n_, :], Vtm[:, n_, hh * 128:(hh + 1) * 128],
                                       True, True, [RKz[di], R_A8], [rps])
                                for qi, n_ in enumerate(grp):
                                    nxt = n_ + 1 if di == 0 else n_ - 1
                                    si = (k4 + qi) % 2
                                    if cur is None:
                                        CP(st32[si][rows, :], ps[rows, qi * 128:(qi + 1) * 128], [rps], [R_st32[si]])
                                    else:
                                        STT(st32[si][rows, :], st32[1 - si][rows, :], CD[rows, hh * 2 + di:hh * 2 + di + 1],
                                            ps[rows, qi * 128:(qi + 1) * 128], ALU.mult, ALU.add,
                                            [rps, R_st32[1 - si], R_rtab], [R_st32[si]])
                                    cur = si
                                    ACT(Sst[rows, di, nxt, :], st32[si][rows, :], AF.Copy, [R_st32[si]], [R_A7])
                        if stop == "ret5":
                            raise _Stop()
                        for tb in range(NTB):
                            cols = slice(tb * TB, (tb + 1) * TB)
                            pss, rpss = PS()
                            for cq in range(4):
                                ct = slice(tb * TB + cq * 128, tb * TB + (cq + 1) * 128)
                                MM(pss[:, cq * 128:(cq + 1) * 128], Krot[rows, ct], Qrot[rows, ct], True, True,
                                   [R_A1, R_A2], [rpss])
                            PT, rpt = T16()
                            TT(PT[:].rearrange("p (c i) -> p c i", i=128), pss[:].rearrange("p (c i) -> p c i", i=128),
                               DT[:, hh, :].unsqueeze(1).to_broadcast([128, 4, 128]), ALU.mult, [rpss, R_rtab], [rpt])
                            qx = []
                            for di in range(2):
                                qt, rqt = T16()
                                TT(qt[rows, :].rearrange("p (c i) -> p c i", i=128),
                                   Qrot[rows, cols].rearrange("p (c i) -> p c i", i=128),
                                   XI[rows, hh, di, :].unsqueeze(1).to_broadcast([64, 4, 128]), ALU.mult,
                                   [R_A1, R_rtab], [rqt])
                                qx.append((qt, rqt))
                            psy, rpsy = PS()
                            for cq in range(4):
                                n_ = tb * 4 + cq
                                osl = psy[:, cq * 128:(cq + 1) * 128]
                                terms = [(Vtm[:, n_, hh * 128:(hh + 1) * 128], PT[:, cq * 128:(cq + 1) * 128], [R_A8, rpt])]
                                if n_ > 0:
                                    terms.append((Sst[rows, 0, n_, :], qx[0][0][rows, cq * 128:(cq + 1) * 128], [R_A7, qx[0][1]]))
                                if n_ < 31:
                                    terms.append((Sst[rows, 1, n_, :], qx[1][0][rows, cq * 128:(cq + 1) * 128], [R_A7, qx[1][1]]))
                                for ti, (lt, rh, rd) in enumerate(terms):
                                    MM(osl, lt, rh, ti == 0, ti == len(terms) - 1, rd, [rpsy])
                            ybf, rybf = T16()
                            ACT(ybf[:], psy[:], AF.Copy, [rpsy], [rybf])
                            ysq, rysq = T16()
                            ACT(ysq[:], psy[:], AF.Square, [rpsy], [rysq])
                            psm, rpsm = PS()
                            MM(psm[:], onesgn[:], ybf[:], True, True, [rybf, R_const], [rpsm])
                            psq, rpsq = PS()
                            MM(psq[:], onesgn[:], ysq[:], True, True, [rysq, R_const], [rpsq])
                            m2, rm2 = T32()
                            ACT(m2[:], psm[:], AF.Square, [rpsm], [rm2])
                            var, rvar = T32()
                            TT(var[:], psq[:], m2[:], ALU.subtract, [rpsq, rm2], [rvar])
                            ACT(var[:], var[:], AF.Sqrt, [], [rvar], bias=GN_EPS)
                            RECIP(var[:], var[:], [], [rvar])
                            mean32, rmean = T32()
                            ACT(mean32[:], psm[:], AF.Copy, [rpsm], [rmean])
                            yc, ryc = T32()
                            TT(yc[:], psy[:], mean32[:], ALU.subtract, [rpsy, rmean], [ryc])
                            TT(yc[:], yc[:], var[:], ALU.mult, [rvar], [ryc])
                            psg, rpsg = PS()
                            for kc in range(8):
                                MM(psg[:], wg[:, kc, hh * 128:(hh + 1) * 128], uT[:, kc, cols], kc == 0, kc == 7,
                                   [rwgb, R_uT[tb]], [rpsg])
                            gs, rgs = T32()
                            ACT(gs[:], psg[:], AF.Silu, [rpsg], [rgs])
                            ot, rot_ = T16()
                            TT(ot[:], yc[:], gs[:], ALU.mult, [ryc, rgs], [rot_])
                            DMA("sp", retS_d[head][:, cols], ot[:], C_misc[3 + tb % 2], [rot_], [R_retS[tb]])

                if stop == "ret":
                    raise _Stop()
                u2T = A1[:, :].rearrange("p (k t) -> p k t", t=512)
                mT = A2[:, :].rearrange("p (k t) -> p k t", t=512)
                attblk = A3[:, :].rearrange("p (k t) -> p k t", t=512)
                retblk = A4[:, :].rearrange("p (k t) -> p k t", t=512)
                hid = [A5[:, 0:4096].rearrange("p (k t) -> p k t", t=512), A6[:, 0:4096].rearrange("p (k t) -> p k t", t=512)]
                R_hid = [R_A5, R_A6]
                last = (l == NL - 1)
                for tb in range(NTB):
                    cols = slice(tb * TB, (tb + 1) * TB)
                    DMA("sp", hblk, hT_v[:, :, cols], C_h[0], [R_hT[tb]], [R_A7])
                    DMA("sp", attblk[0:64], attS_d[:, :, cols].rearrange("h d t -> d h t"), C_misc[0], [R_attS[tb]], [R_A3])
                    DMA("sp", retblk, retS_d[:, :, cols].rearrange("h d t -> d h t"), C_misc[3], [R_retS[tb]], [R_A4])
                    for fc in range(8):
                        if True:
                            wm, rwm = WNEXT()
                            vm = wm[:, :].rearrange("p (k f) -> p k f", f=512)
                            psa, rpsa = PS()
                            for kc in range(8):
                                MM(psa[:], vm[:, kc, 0:128], uT[:, kc, cols], kc == 0, kc == 7, [rwm, R_uT[tb]], [rpsa])
                            sa, rsa = T32()
                            ACT(sa[:], psa[:], AF.Sigmoid, [rpsa], [rsa])
                            psb_, rpsb = PS()
                            for kc in range(8):
                                MM(psb_[:], vm[:, kc, 128:256], uT[:, kc, cols], kc == 0, kc == 7, [rwm, R_uT[tb]], [rpsb])
                            sb_, rsb = T32()
                            ACT(sb_[:], psb_[:], AF.Sigmoid, [rpsb], [rsb])
                            pa, rpa = PS()
                            for h in range(8):
                                MM(pa[:], vm[0:64, h, 256:384], attblk[0:64, h, :], h == 0, h == 7, [rwm, R_A3], [rpa])
                            pr, rpr = PS()
                            for h in range(8):
                                MM(pr[:], vm[:, h, 384:512], retblk[:, h, :], h == 0, h == 7, [rwm, R_A4], [rpr])
                            TT(sa[:], sa[:], pa[:], ALU.mult, [rpa], [rsa])
                            TT(sb_[:], sb_[:], pr[:], ALU.mult, [rpr], [rsb])
                            TT(mT[:, fc, :], sa[:], sb_[:], ALU.add, [rsa, rsb], [R_A2])
                    for half in range(2):
                        wo, rwo = WNEXT()
                        vo = wo[:, :].rearrange("p (k f) -> p k f", f=512)
                        for f4 in range(4):
                            fc = half * 4 + f4
                            ps, rps = PS()
                            for kc in range(8):
                                MM(ps[:], vo[:, kc, f4 * 128:(f4 + 1) * 128], mT[:, kc, :], kc == 0, kc == 7, [rwo, R_A2], [rps])
                            ACT(zblk[:, fc, :], ps[:], AF.Copy, [rps], [R_A8])
                    postnorm_add(zfn, [R_A8], l, 1, hfn, [R_A7])
                    prenorm(hfn, [R_A7], l, 2, lambda kc: u2T[:, kc, :], [R_A1])
                    for half in range(2):
                        for q4 in range(4):
                            wu, rwu = WNEXT()
                            vu = wu[:, :].rearrange("p (k f) -> p k f", f=512)
                            for f4 in range(4):
                                hc = q4 * 4 + f4
                                ps, rps = PS()
                                for kc in range(8):
                                    MM(ps[:], vu[:, kc, f4 * 128:(f4 + 1) * 128], u2T[:, kc, :], kc == 0, kc == 7,
                                       [rwu, R_A1], [rps])
                                rl, rrl = T32()
                                ACT(rl[:], ps[:], AF.Relu, [rps], [rrl])
                                TT(hid[hc // 8][:, hc % 8, :], rl[:], rl[:], ALU.mult, [rrl], [R_hid[hc // 8]])
                        for q4 in range(4):
                            wd, rwd = WNEXT()
                            vd = wd[:, :].rearrange("p (k f) -> p k f", f=256)
                            for f2 in range(2):
                                fc = q4 * 2 + f2
                                ps, rps = PS()
                                for hc in range(16):
                                    MM(ps[:], vd[:, hc, f2 * 128:(f2 + 1) * 128], hid[hc // 8][:, hc % 8, :], hc == 0, hc == 15,
                                       [rwd, R_hid[hc // 8]], [rps])
                                if half == 0:
                                    ACT(zblk[:, fc, :], ps[:], AF.Copy, [rps], [R_A8])
                                else:
                                    TT(zblk[:, fc, :], ps[:], zblk[:, fc, :], ALU.add, [rps], [R_A8])
                    postnorm_add(zfn, [R_A8], l, 3, hfn, [R_A7])
                    prenorm(hfn, [R_A7], l, None, lambda kc: u2T[:, kc, :], [R_A1])
                    pf, rpf = T32(); pf2, rpf2 = T32()
                    for i2, (pt_, rp_) in enumerate(((pf, rpf), (pf2, rpf2))):
                        DMA("sp", pt_[:].rearrange("p (t f) -> p t f", f=256),
                            p_d[l][tb * TB + i2 * 256:tb * TB + (i2 + 1) * 256, :].rearrange("(t p) f -> p t f", p=128),
                            C_misc[5], [], [rp_])
                    pTt = []
                    for fk in range(2):
                        ps, rps = PS()
                        for tt in range(4):
                            src_t, rsrc = ((pf, rpf), (pf2, rpf2))[tt // 2]
                            TR(ps[:, tt * 128:(tt + 1) * 128],
                               src_t[:].rearrange("p (t f) -> p t f", f=256)[:, tt % 2, fk * 128:(fk + 1) * 128], ident32,
                               [rsrc, R_misc], [rps])
                        pT_, rpT = T16()
                        ACT(pT_[:], ps[:], AF.Copy, [rps], [rpT])
                        pTt.append((pT_, rpT))
                    wpp, rwpp = WNEXT()
                    vpp = wpp[:, 0:2048].rearrange("p (k f) -> p k f", f=1024)
                    for fc in range(8):
                        ps, rps = PS()
                        for kc in range(2):
                            MM(ps[:], vpp[:, kc, fc * 128:(fc + 1) * 128], pTt[kc][0][:], kc == 0, kc == 1,
                               [rwpp, pTt[kc][1]], [rps])
                        ACT(zblk[:, fc, :], ps[:], AF.Copy, [rps], [R_A8])
                    rs, rrs = rms_rstd(zfn, [R_A8])
                    for half in range(2):
                        wpg, rwpg = WNEXT()
                        vpg = wpg[:, :].rearrange("p (k f) -> p k f", f=512)
                        for f4 in range(4):
                            fc = half * 4 + f4
                            ps, rps = PS()
                            for kc in range(8):
                                MM(ps[:], vpg[:, kc, f4 * 128:(f4 + 1) * 128], u2T[:, kc, :], kc == 0, kc == 7, [rwpg, R_A1], [rps])
                            gsg, rgsg = T32()
                            ACT(gsg[:], ps[:], AF.Sigmoid, [rps], [rgsg])
                            tmp, rt = T32()
                            STT(tmp[:], zblk[:, fc, :], gcol(l, 4, fc), rs[:], ALU.mult, ALU.mult, [R_A8, rrs, R_const], [rt])
                            TT(tmp[:], tmp[:], gsg[:], ALU.mult, [rgsg], [rt])
                            TT(hblk[:, fc, :], hblk[:, fc, :], tmp[:], ALU.add, [rt], [R_A7])
                    if not last:
                        DMA("sp", hT_v[:, :, cols], hblk, C_h[1], [R_A7], [R_hT[tb]])
                        prenorm(hfn, [R_A7], l + 1, 0, lambda kc: uT[:, kc, cols], [R_uT[tb]])
                    else:
                        oblk = A8[:, :].rearrange("p (t f) -> p t f", f=1024)
                        for tt in range(4):
                            for k2 in range(2):
                                ps, rps = PS()
                                for kq in range(4):
                                    kc = k2 * 4 + kq
                                    TR(ps[:, kq * 128:(kq + 1) * 128], hblk[:, kc, tt * 128:(tt + 1) * 128], ident32,
                                       [R_A7, R_misc], [rps])
                                ACT(oblk[:, tt, k2 * 512:(k2 + 1) * 512], ps[:], AF.Copy, [rps], [R_A8])
                        it = DMA("sp", out_d[cols, :].rearrange("(t p) f -> p t f", p=128), oblk, C_out[tb % 4], [R_A8], [])
                        out_dmas.append(it)

        except _Stop:
            pass

        fence = S.add("sp", lambda e: e.nop(), reads=[], writes=[])
        for ch in S.chans:
            if ch.last is not None and ch.last not in fence.deps:
                fence.deps.append(ch.last)

        nep = S.finalize()
        esems = {e: [es.enter_context(nc.semaphore(f"{e}{k}")) for k in range(nep[e])] for e in S.ENGS}
        csems = [es.enter_context(nc.semaphore(f"c{i}")) for i in range(len(S.chans))]
        with nc.Block() as block:
            S.emit(nc, block, esems, csems)
        nc._sched = S
    return nc


FUSED = False
_PROG = {}


def _get_prog(NL):
    if NL not in _PROG:
        _PROG[NL] = build_program(NL)
    return _PROG[NL]


def _gains_layout(gl):
    NL = gl.shape[0]
    return np.ascontiguousarray(gl.reshape(NL, 5, 8, 128).transpose(3, 0, 1, 2).reshape(128, NL * 40))


def kernel(x, p, w_in, w_att_out, w_ret_out, w_out, w_mlp_up, w_mlp_down, w_ple_gate, w_ple_proj,
           ret_decay_logit, norm_mix_pre, norm_mix_post, norm_mlp_pre, norm_mlp_post, norm_ple):
    f = lambda a: np.ascontiguousarray(np.asarray(a, dtype=np.float32))
    x = f(x); p = f(p)
    W = dict(w_in=f(w_in), w_att_out=f(w_att_out), w_ret_out=f(w_ret_out), w_out=f(w_out),
             w_mlp_up=f(w_mlp_up), w_mlp_down=f(w_mlp_down), w_ple_gate=f(w_ple_gate), w_ple_proj=f(w_ple_proj))
    dec = f(ret_decay_logit).reshape(-1, 16)
    gl = np.stack([f(norm_mix_pre), f(norm_mix_post), f(norm_mlp_pre), f(norm_mlp_post), f(norm_ple)], axis=1)
    rope, misc = host_consts()
    B = x.shape[0]
    depth = W["w_in"].shape[0]
    groups = [list(range(depth))] if FUSED else [[l] for l in range(depth)]
    h = x
    for ls in groups:
        nc = _get_prog(len(ls))
        in_maps = []
        for b in range(B):
            m = {k: np.ascontiguousarray(v[ls]) for k, v in W.items()}
            m["x"] = np.ascontiguousarray(h[b])
            m["p"] = np.ascontiguousarray(p[ls, b])
            m["dec"] = np.ascontiguousarray(dec[ls])
            m["gains"] = _gains_layout(gl[ls])
            m["rope"] = rope
            m["misc"] = misc
            in_maps.append(m)
        res = run_bass_kernel_spmd(nc, in_maps, core_ids=list(range(B)))
        h = np.stack([np.asarray(r["out"], dtype=np.float32) for r in res.results], axis=0)
    return h
```

```python
import numpy as np
import concourse.bass as bass
import concourse.mybir as mybir
from concourse.bass_utils import run_bass_kernel_spmd

F32 = mybir.dt.float32
BF16 = mybir.dt.bfloat16
ALU = mybir.AluOpType
AF = mybir.ActivationFunctionType


EPOCH = 30000


class Res:
    __slots__ = ("w", "r", "name", "excl")

    def __init__(self, name="", excl=False):
        self.w = None
        self.r = []
        self.name = name
        self.excl = excl


class Item:
    __slots__ = ("eng", "fn", "deps", "marked", "chan", "didx", "cpos", "val")

    def __init__(self, eng, fn):
        self.eng = eng
        self.fn = fn
        self.deps = []
        self.marked = False
        self.chan = None
        self.didx = 0
        self.cpos = 0
        self.val = 0


class Chan:
    __slots__ = ("n", "last", "sem", "id")

    def __init__(self, i):
        self.n = 0
        self.last = None
        self.sem = None
        self.id = i


class Sched:
    ENGS = ("pe", "act", "dve", "pool", "sp")

    def __init__(self):
        self.streams = {e: [] for e in self.ENGS}
        self.ncomp = {e: 0 for e in self.ENGS}
        self.chans = []

    def chan(self):
        c = Chan(len(self.chans))
        self.chans.append(c)
        return c

    def add(self, eng, fn, reads=(), writes=(), chan=None):
        it = Item(eng, fn)
        it.cpos = self.ncomp[eng]
        raw = []
        ex = [r for r in reads if r.excl]
        if ex:
            reads = [r for r in reads if not r.excl]
            writes = list(writes) + [r for r in ex if r not in writes]
        for r in reads:
            if r.w is not None:
                raw.append(r.w)
        for w in writes:
            if w.w is not None:
                raw.append(w.w)
            raw.extend(w.r)
        if chan is not None:
            it.chan = chan
            chan.n += 1
            it.didx = chan.n
            if chan.last is not None:
                raw.append(chan.last)
            chan.last = it
        seen = set()
        for d in raw:
            if id(d) in seen or d is it:
                continue
            seen.add(id(d))
            if d.chan is not None:
                it.deps.append(d)
                continue
            if d.eng == eng:
                if eng == "pe" and chan is None:
                    continue
                if chan is None and (self.ncomp[eng] - d.cpos) > 2:
                    continue
            d.marked = True
            it.deps.append(d)
        for r in reads:
            r.r.append(it)
        for w in writes:
            w.w = it
            w.r = []
        if chan is None:
            self.ncomp[eng] += 1
        self.streams[eng].append(it)
        return it

    def finalize(self):
        self.nmarked = {}
        for e in self.ENGS:
            v = 0
            for it in self.streams[e]:
                if it.chan is None and it.marked:
                    v += 1
                    it.val = v
            self.nmarked[e] = v
        return {e: max(1, -(-self.nmarked[e] // EPOCH)) for e in self.ENGS}

    def emit(self, nc, block, esems, csems):
        self.nwaits = 0
        self.log = {e: [] for e in self.ENGS}
        for c in self.chans:
            c.sem = csems[c.id]
        engobj = {"pe": block.tensor, "act": block.scalar, "dve": block.vector,
                  "pool": block.gpsimd, "sp": block.sync}

        def run(e):
            def body(eng):
                waited = {}
                for it in self.streams[e]:
                    need = {}
                    for d in it.deps:
                        if d.chan is not None:
                            key = ("c", d.chan.id)
                            sem = d.chan.sem
                            val = 16 * d.didx
                        else:
                            ep = (d.val - 1) // EPOCH
                            key = ("e", d.eng, ep)
                            sem = esems[d.eng][ep]
                            val = (d.val - 1) % EPOCH + 1
                        if waited.get(key, 0) < val and need.get(key, (None, 0))[1] < val:
                            need[key] = (sem, val)
                    for key, (sem, val) in need.items():
                        eng.wait_ge(sem, val)
                        waited[key] = val
                        self.nwaits += 1
                        self.log[e].append(f"   wait {key} >= {val}")
                    self.log[e].append(f"{'DMA' if it.chan is not None else 'op'} val={it.val if it.chan is None else ('c%d:%d' % (it.chan.id, 16*it.didx))} marked={it.marked} tag={getattr(it.fn, '_tag', '')}")
                    ins = it.fn(eng)
                    if it.chan is not None:
                        ins.then_inc(it.chan.sem, 16)
                    elif it.marked:
                        ins.then_inc(esems[e][(it.val - 1) // EPOCH], 1)
            engobj[e](body)

        for e in self.ENGS:
            if self.streams[e]:
                run(e)


from contextlib import ExitStack
import ml_dtypes
import os
DBG = os.environ.get("KDBG", "").split(",")

SEQ = 4096
DM = 1024
TB = 512
NTB = SEQ // TB
QA0, KA0, VA0 = 0, 512, 1024
QR0, KR0, VR0, GR0 = 1536, 2048, 2560, 3584
GA0, GB0 = 4608, 5632
EPS = 1e-6
GN_EPS = 1e-5
NRING = 3


def host_consts():
    pos = np.arange(SEQ, dtype=np.float32)
    inv_a = (np.float32(500000.0) ** (-(np.arange(8, dtype=np.float32) * 2.0 / 16.0))).astype(np.float32)
    ang_a = pos[None, :] * inv_a[:, None]
    CA = np.ones((128, SEQ), np.float32)
    SA = np.zeros((128, SEQ), np.float32)
    for hh in range(2):
        b = hh * 64
        CA[b:b + 8] = np.cos(ang_a)
        CA[b + 8:b + 16] = np.cos(ang_a)
        SA[b:b + 8] = -np.sin(ang_a)
        SA[b + 8:b + 16] = np.sin(ang_a)
    inv_r = (np.float32(10000.0) ** (-(np.arange(32, dtype=np.float32) * 2.0 / 64.0))).astype(np.float32)
    ang_r = pos[None, :] * inv_r[:, None]
    CR = np.zeros((128, SEQ), np.float32)
    SR = np.zeros((128, SEQ), np.float32)
    for hh in range(2):
        b = hh * 64
        CR[b:b + 32] = np.cos(ang_r)
        CR[b + 32:b + 64] = np.cos(ang_r)
        SR[b:b + 32] = -np.sin(ang_r)
        SR[b + 32:b + 64] = np.sin(ang_r)
    rope = np.stack([CA, SA, CR, SR]).astype(np.float32)
    permA = np.zeros((128, 128), np.float32)
    permR = np.zeros((128, 128), np.float32)
    for hh in range(2):
        b = hh * 64
        for d in range(8):
            permA[b + d + 8, b + d] = 1.0
            permA[b + d, b + d + 8] = 1.0
        for d in range(32):
            permR[b + d + 32, b + d] = 1.0
            permR[b + d, b + d + 32] = 1.0
    pp = np.arange(128)[:, None]
    xx = np.arange(256)[None, :]
    maskb = np.where((xx >= pp) & (xx <= pp + 128), 0.0, -30000.0).astype(np.float32)
    jj = np.arange(128)[:, None].astype(np.float32)
    ii = np.arange(128)[None, :].astype(np.float32)
    BIG = 2.0e6
    distF = np.where(ii >= jj, ii - jj, BIG).astype(np.float32)
    distB = np.where(jj > ii, jj - ii, BIG).astype(np.float32)
    zexp = np.zeros((128, 4), np.float32)
    zexp[:, 0] = 127.0 - np.arange(128)
    zexp[:, 1] = np.arange(128)
    zexp[:, 2] = 128.0
    xexp = np.zeros((128, 2, 128), np.float32)
    xexp[:, 0, :] = np.arange(128)[None, :] + 1.0
    xexp[:, 1, :] = 128.0 - np.arange(128)[None, :]
    misc = np.concatenate([np.eye(128, dtype=np.float32), permA, permR, maskb, distF, distB,
                           zexp, xexp.reshape(128, 256)], axis=1)
    return rope, np.ascontiguousarray(misc)


MISC_W = 128 * 3 + 256 + 128 * 2 + 4 + 256


class _Stop(Exception):
    pass


def build_program(NL, stop=None):
    nc = bass.Bass("TRN2", target_bir_lowering=False)

    def din(name, shape, dt=F32):
        return nc.dram_tensor(name, shape, dt, kind="ExternalInput").ap()

    x_d = din("x", [SEQ, DM])
    p_d = din("p", [NL, SEQ, 256])
    w_in_d = din("w_in", [NL, DM, 6656])
    w_ao_d = din("w_att_out", [NL, 512, DM])
    w_ro_d = din("w_ret_out", [NL, 1024, DM])
    w_out_d = din("w_out", [NL, DM, DM])
    w_up_d = din("w_mlp_up", [NL, DM, 4096])
    w_dn_d = din("w_mlp_down", [NL, 4096, DM])
    w_pg_d = din("w_ple_gate", [NL, DM, DM])
    w_pp_d = din("w_ple_proj", [NL, 256, DM])
    dec_d = din("dec", [NL, 16])
    gains_d = din("gains", [128, NL * 5 * 8])
    rope_d = din("rope", [4, 128, SEQ])
    misc_d = din("misc", [128, MISC_W])
    out_d = nc.dram_tensor("out", [SEQ, DM], F32, kind="ExternalOutput").ap()
    hT_d = nc.dram_tensor("hT_scr", [DM, SEQ], F32).ap()
    _dk = dict(kind="ExternalOutput") if "dump" in DBG else {}
    attS_d = nc.dram_tensor("att_scr", [8, 64, SEQ], BF16, **_dk).ap()
    retS_d = nc.dram_tensor("ret_scr", [8, 128, SEQ], BF16, **_dk).ap()
    hT_v = hT_d.rearrange("(kc p) t -> p kc t", p=128)
    R_hT = [Res(f"hT{i}") for i in range(NTB)]
    R_attS = [Res(f"attS{i}") for i in range(NTB)]
    R_retS = [Res(f"retS{i}") for i in range(NTB)]

    es = ExitStack()
    with es:
        def sb(name, shp, dt):
            return es.enter_context(nc.sbuf_tensor("sb_" + name, shp, dt))

        def pst(name, shp, dt):
            return es.enter_context(nc.psum_tensor("pp_" + name, shp, dt))

        S = Sched()
        uT = sb("uT", [128, 8, SEQ], BF16)
        R_uT = [Res(f"uT{i}") for i in range(NTB)]
        misc32 = sb("misc32", [128, MISC_W], F32)
        miscbf = sb("miscbf", [128, 640], BF16)
        R_misc = Res("misc")
        gains = sb("gains", [128, NL * 5 * 8], F32)
        onesbf = sb("onesbf", [128, 128], BF16)
        onesgn = sb("onesgn", [128, 128], BF16)
        ones32 = sb("ones32", [128, 64], F32)
        R_const = Res("const")
        ident32 = misc32[:, 0:128]
        o_ = 384 + 256
        distF = misc32[:, o_:o_ + 128]
        distB = misc32[:, o_ + 128:o_ + 256]
        zexp = misc32[:, o_ + 256:o_ + 260]
        xexp = misc32[:, o_ + 260:o_ + 516].rearrange("p (k i) -> p k i", k=2)
        identbf = miscbf[:, 0:128]
        permAbf = miscbf[:, 128:256]
        permRbf = miscbf[:, 256:384]
        maskbf = miscbf[:, 384:640]

        ring = [sb(f"ring{i}", [128, 4096], BF16) for i in range(NRING)]
        R_ring = [Res(f"ring{i}") for i in range(NRING)]
        C_ring = [S.chan() for _ in range(NRING)]

        A1 = sb("A1", [128, 4096], BF16); R_A1 = Res("A1")
        A2 = sb("A2", [128, 4096], BF16); R_A2 = Res("A2")
        A3 = sb("A3", [128, 4096], BF16); R_A3 = Res("A3")
        A4 = sb("A4", [128, 4096], BF16); R_A4 = Res("A4")
        A5 = sb("A5", [128, 4160], BF16); R_A5 = Res("A5")
        A6 = sb("A6", [128, 4160], BF16); R_A6 = Res("A6")
        A7 = sb("A7", [128, 4096], F32); R_A7 = Res("A7")
        A8 = sb("A8", [128, 4096], F32); R_A8 = Res("A8")
        NT32 = 6
        t32 = [sb(f"t32_{i}", [128, 512], F32) for i in range(NT32)]
        R_t32 = [Res(f"t32_{i}") for i in range(NT32)]
        NTB16 = 6
        t16 = [sb(f"t16_{i}", [128, 512], BF16) for i in range(NTB16)]
        R_t16 = [Res(f"t16_{i}") for i in range(NTB16)]
        ropeC = sb("ropeC", [128, 512], F32); ropeS = sb("ropeS", [128, 512], F32)
        R_rope = Res("rope")
        C_rope = [S.chan(), S.chan()]
        dect = sb("dect", [128, 1, 16], F32)
        lg = sb("lg", [128, 16], F32)
        rtab = sb("rtab", [128, 2 * 128 + 4 * 128 + 8], F32)
        R_dec = Res("dec"); R_rtab = Res("rtab")
        st32 = [sb(f"st32_{i}", [128, 128], F32) for i in range(2)]
        R_st32 = [Res("st32a"), Res("st32b")]

        rsb_ = [sb(f"rs_{i}", [128, 512], F32) for i in range(2)]
        R_rsb = [Res("rs0"), Res("rs1")]
        cnt = {"t32": 0, "t16": 0, "ps": 0, "rs": 0}
        if "verbose" in DBG:
            print("SBUF bytes remaining after alloc:", nc.sbuf_bytes_remaining)

        def T32():
            i = cnt["t32"] % NT32; cnt["t32"] += 1
            return t32[i], R_t32[i]

        def T16():
            i = cnt["t16"] % NTB16; cnt["t16"] += 1
            return t16[i], R_t16[i]

        psb = [pst(f"ps{i}", [128, 512], F32) for i in range(8)]
        R_ps = [Res(f"ps{i}", excl=True) for i in range(8)]

        def PS():
            i = cnt["ps"] % 6; cnt["ps"] += 1
            return psb[i], R_ps[i]

        def MM(out, lhsT, rhs, start, stop, rd, wr):
            return S.add("pe", lambda e: e.matmul(out, lhsT, rhs, start=start, stop=stop), reads=rd, writes=wr)

        def TR(out, in_, ident, rd, wr):
            return S.add("pe", lambda e: e.transpose(out, in_, ident), reads=rd, writes=wr)

        def ACT(out, in_, func, rd, wr, scale=1.0, bias=0.0):
            return S.add("act", lambda e: e.activation(out, in_, func, bias=bias, scale=scale), reads=rd, writes=wr)

        def TT(out, a, b, op, rd, wr, eng="dve"):
            return S.add(eng, lambda e: e.tensor_tensor(out, a, b, op), reads=rd, writes=wr)

        def STT(out, in0, scalar, in1, op0, op1, rd, wr, eng="dve"):
            return S.add(eng, lambda e: e.scalar_tensor_tensor(out, in0, scalar, in1, op0, op1), reads=rd, writes=wr)

        def TS(out, in0, s1, s2, op0, op1, rd, wr, eng="dve"):
            if s2 is None:
                return S.add(eng, lambda e: e.tensor_scalar(out, in0, s1, None, op0), reads=rd, writes=wr)
            return S.add(eng, lambda e: e.tensor_scalar(out, in0, s1, s2, op0, op1), reads=rd, writes=wr)

        def CP(out, in_, rd, wr, eng="dve"):
            return S.add(eng, lambda e: e.tensor_copy(out, in_), reads=rd, writes=wr)

        def RECIP(out, in_, rd, wr):
            return S.add("dve", lambda e: e.reciprocal(out, in_), reads=rd, writes=wr)

        def MEMSET(ap, val, wr, eng="dve"):
            return S.add(eng, lambda e: e.memset(ap, val), writes=wr)

        def DMA(q, out, in_, chan, rd, wr):
            return S.add(q, lambda e: e.dma_start(out=out, in_=in_), reads=rd, writes=wr, chan=chan)

        out_dmas = []

        def ss(start, count, step):
            return slice(start, start + step * (count - 1) + 1, step)

        wq = {"specs": [], "issued": 0, "used": 0, "base": 0}

        def w_issue_upto(k):
            while wq["issued"] < min(k, len(wq["specs"])):
                i = wq["issued"]
                b = i % NRING
                for (dst_fn, src) in wq["specs"][i]:
                    DMA("pool", dst_fn(ring[b]), src, C_ring[b], [], [R_ring[b]])
                wq["issued"] += 1

        def WNEXT(keep=False):
            i = wq["used"]
            if not keep:
                wq["base"] = i
            w_issue_upto(wq["base"] + NRING)
            wq["used"] += 1
            b = i % NRING
            return ring[b], R_ring[b]

        def panel(src_ap, kcn, ncols, col0=0, width=None, parts=128):
            width = width or ncols

            def dst(rb):
                return rb[0:parts, 0:kcn * width].rearrange("p (k f) -> p k f", f=width)[:, :, col0:col0 + ncols]
            return (dst, src_ap)

        def attn_specs(l, hp):
            wv = w_in_d[l].rearrange("(kc p) f -> p kc f", p=128)
            return [[panel(wv[:, :, QA0 + hp * 128:QA0 + hp * 128 + 128], 8, 128, 0, 384),
                     panel(wv[:, :, KA0 + hp * 128:KA0 + hp * 128 + 128], 8, 128, 128, 384),
                     panel(wv[:, :, VA0 + hp * 128:VA0 + hp * 128 + 128], 8, 128, 256, 384)]]

        def ret_specs(l, hp):
            wv = w_in_d[l].rearrange("(kc p) f -> p kc f", p=128)
            return [[panel(wv[:, :, QR0 + hp * 128:QR0 + hp * 128 + 128], 8, 128, 0, 512),
                     panel(wv[:, :, KR0 + hp * 128:KR0 + hp * 128 + 128], 8, 128, 128, 512),
                     panel(wv[:, :, VR0 + hp * 256:VR0 + hp * 256 + 256], 8, 256, 256, 512)],
                    [panel(wv[:, :, GR0 + hp * 256:GR0 + hp * 256 + 256], 8, 256, 0, 256)]]

        def phc_specs(l):
            wv = w_in_d[l].rearrange("(kc p) f -> p kc f", p=128)
            sp = []
            wao = w_ao_d[l].rearrange("(h p) f -> p h f", p=64)
            wro = w_ro_d[l].rearrange("(kc p) f -> p kc f", p=128)
            for fc in range(8):
                c0 = fc * 128
                sp.append([panel(wv[:, :, GA0 + c0:GA0 + c0 + 128], 8, 128, 0, 512),
                           panel(wv[:, :, GB0 + c0:GB0 + c0 + 128], 8, 128, 128, 512),
                           panel(wao[:, :, c0:c0 + 128], 8, 128, 256, 512, parts=64),
                           panel(wro[:, :, c0:c0 + 128], 8, 128, 384, 512)])
            wo = w_out_d[l].rearrange("(kc p) f -> p kc f", p=128)
            for half in range(2):
                sp.append([panel(wo[:, :, half * 512:half * 512 + 512], 8, 512)])
            wu = w_up_d[l].rearrange("(kc p) f -> p kc f", p=128)
            wd = w_dn_d[l].rearrange("(kc p) f -> p kc f", p=128)
            for half in range(2):
                for q4 in range(4):
                    c0 = half * 2048 + q4 * 512
                    sp.append([panel(wu[:, :, c0:c0 + 512], 8, 512)])
                for q4 in range(4):
                    sp.append([panel(wd[:, half * 16:half * 16 + 16, q4 * 256:q4 * 256 + 256], 16, 256)])
            wpp = w_pp_d[l].rearrange("(kc p) f -> p kc f", p=128)
            sp.append([panel(wpp[:, :, :], 2, 1024)])
            wpg = w_pg_d[l].rearrange("(kc p) f -> p kc f", p=128)
            for half in range(2):
                sp.append([panel(wpg[:, :, half * 512:half * 512 + 512], 8, 512)])
            return sp

        for l in range(NL):
            for hp in range(4):
                wq["specs"] += attn_specs(l, hp)
            for hp in range(4):
                wq["specs"] += ret_specs(l, hp)
            for tb in range(NTB):
                wq["specs"] += phc_specs(l)

        C_c = [S.chan() for _ in range(4)]
        DMA("sp", misc32[:], misc_d, C_c[0], [], [R_misc])
        DMA("sp", gains[:], gains_d, C_c[1], [], [R_const])
        DMA("pool", miscbf[:], misc_d[:, 0:640], C_c[2], [], [R_misc])
        MEMSET(onesbf[:], 1.0, [R_const])
        MEMSET(onesgn[:], 1.0 / 128.0, [R_const])
        MEMSET(ones32[:], 1.0, [R_const])

        def gcol(l, which, kc):
            c = (l * 5 + which) * 8 + kc
            return gains[:, c:c + 1]

        def rms_rstd(src_fn, src_res, n_feat_chunks=8, inv_n=1.0 / 1024.0):
            ps, rps = PS()
            for kc in range(n_feat_chunks):
                sq, rsq = T16()
                ACT(sq[:], src_fn(kc), AF.Square, src_res, [rsq])
                MM(ps[:], onesbf[:], sq[:], kc == 0, kc == n_feat_chunks - 1, [rsq, R_const], [rps])
            i_ = cnt["rs"] % 2; cnt["rs"] += 1
            rs, rrs = rsb_[i_], R_rsb[i_]
            ACT(rs[:], ps[:], AF.Sqrt, [rps], [rrs], scale=inv_n, bias=EPS)
            RECIP(rs[:], rs[:], [], [rrs])
            return rs, rrs

        def prenorm(hblk_fn, hres, l, which, dst_fn, dres):
            rs, rrs = rms_rstd(hblk_fn, hres)
            for kc in range(8):
                if which is None:
                    TT(dst_fn(kc), hblk_fn(kc), rs[:], ALU.mult, hres + [rrs], dres)
                else:
                    STT(dst_fn(kc), hblk_fn(kc), gcol(l, which, kc), rs[:], ALU.mult, ALU.mult,
                        hres + [rrs, R_const], dres)

        def postnorm_add(z_fn, zres, l, which, h_fn, hres):
            rs, rrs = rms_rstd(z_fn, zres)
            for kc in range(8):
                tmp, rt = T32()
                STT(tmp[:], z_fn(kc), gcol(l, which, kc), rs[:], ALU.mult, ALU.mult, zres + [rrs, R_const], [rt])
                TT(h_fn(kc), h_fn(kc), tmp[:], ALU.add, [rt], hres)

        hblk = A7[:, :].rearrange("p (k t) -> p k t", t=512)
        zblk = A8[:, :].rearrange("p (k t) -> p k t", t=512)

        def hfn(kc):
            return hblk[:, kc, :]

        def zfn(kc):
            return zblk[:, kc, :]

        C_h = [S.chan(), S.chan()]
        C_misc = [S.chan() for _ in range(6)]
        C_out = [S.chan() for _ in range(4)]

        try:
            xblk = A8[:, :].rearrange("p (t f) -> p t f", f=1024)
            for tb in range(NTB):
                cols = slice(tb * TB, (tb + 1) * TB)
                DMA("sp", xblk, x_d[cols, :].rearrange("(t p) f -> p t f", p=128), C_h[0], [], [R_A8])
                for kc in range(8):
                    ps, rps = PS()
                    for tt in range(4):
                        TR(ps[:, tt * 128:(tt + 1) * 128], xblk[:, tt, kc * 128:(kc + 1) * 128], ident32,
                           [R_A8, R_misc], [rps])
                    ACT(hblk[:, kc, :], ps[:], AF.Copy, [rps], [R_A7])
                DMA("sp", hT_v[:, :, cols], hblk, C_h[1], [R_A7], [R_hT[tb]])
                prenorm(hfn, [R_A7], 0, 0, lambda kc: uT[:, kc, cols], [R_uT[tb]])

            if stop == "p0":
                raise _Stop()
            Qrot, Krot, VT = A1, A2, A3
            for l in range(NL):
                for hp in range(4):
                    wb, rwb = WNEXT()
                    if "noattn" in DBG:
                        continue
                    wv = wb[:, 0:8 * 384].rearrange("p (k f) -> p k f", f=384)
                    for tb in range(NTB):
                        cols = slice(tb * TB, (tb + 1) * TB)
                        DMA("sp", ropeC[:], rope_d[0][:, cols], C_rope[0], [], [R_rope])
                        DMA("sp", ropeS[:], rope_d[1][:, cols], C_rope[1], [], [R_rope])
                        for wi, (dstT, rdst) in enumerate(((Qrot, R_A1), (Krot, R_A2))):
                            ps, rps = PS()
                            for kc in range(8):
                                MM(ps[:], wv[:, kc, wi * 128:(wi + 1) * 128], uT[:, kc, cols], kc == 0, kc == 7,
                                   [rwb, R_uT[tb]], [rps])
                            qraw, rq = T16()
                            ACT(qraw[:], ps[:], AF.Copy, [rps], [rq])
                            ps2, rps2 = PS()
                            MM(ps2[:], permAbf, qraw[:], True, True, [rq, R_misc], [rps2])
                            t1, rt1 = T32()
                            TT(t1[:], ps[:], ropeC[:], ALU.mult, [rps, R_rope], [rt1])
                            t2, rt2 = T32()
                            TT(t2[:], ps2[:], ropeS[:], ALU.mult, [rps2, R_rope], [rt2])
                            TT(dstT[:, cols], t1[:], t2[:], ALU.add, [rt1, rt2], [rdst])
                        ps, rps = PS()
                        for kc in range(8):
                            MM(ps[:], wv[:, kc, 256:384], uT[:, kc, cols], kc == 0, kc == 7, [rwb, R_uT[tb]], [rps])
                        ACT(VT[:, cols], ps[:], AF.Copy, [rps], [R_A3])
                    accs = ((A7, R_A7), (A8, R_A8))
                    for g, r in enumerate((1, 4, 16)):
                        for hh in range(2):
                            head = hp * 2 + hh
                            rows = slice(hh * 64, hh * 64 + 64)
                            acc, R_acc = accs[hh]
                            Vb, RVb = (A5, R_A5) if g % 2 == 0 else (A6, R_A6)
                            Vaug = Vb[:, 0:4160].rearrange("p (c h d) -> p c h d", h=2, d=65)
                            J = 32 // r
                            L = SEQ // r
                            if hh == 0:
                                MEMSET(Vaug[:, :, :, 64:65], 1.0, [RVb])
                                for j0 in range(0, 32, 4):
                                    ps, rps = PS()
                                    psv = ps[:, 0:256].bitcast(BF16)
                                    for jq in range(4):
                                        j = j0 + jq
                                        c, jj = divmod(j, J)
                                        t0 = c + r * 128 * jj
                                        TR(psv[:, jq * 128:(jq + 1) * 128], VT[:, ss(t0, 128, r)], identbf,
                                           [R_A3, R_misc], [rps])
                                    ACT(Vaug[:, j0:j0 + 4, :, 0:64],
                                        psv.rearrange("p (c h d) -> p c h d", h=2, d=64), AF.Copy, [rps], [RVb])
                            for c in range(r):
                                for jj in range(J):
                                    j = c * J + jj
                                    x0 = 64 if jj == 0 else 0
                                    x1 = 192 if jj == J - 1 else 256
                                    n = x1 - x0
                                    l0 = 128 * jj - 64 + x0
                                    kt0 = c + r * 128 * jj
                                    ktok = ss(kt0, 128, r)
                                    qt0 = c + r * l0
                                    qtok = ss(qt0, n, r)
                                    st, rst = PS()
                                    MM(st[:, 0:n], Krot[rows, ktok], Qrot[rows, qtok], True, False, [R_A1, R_A2], [rst])
                                    MM(st[:, 0:n], identbf, maskbf[:, x0:x1], False, True, [R_misc], [rst])
                                    PT, rpt = T16()
                                    ACT(PT[:, 0:n], st[:, 0:n], AF.Exp, [rst], [rpt], scale=0.125)
                                    na = 128 - x0
                                    for (m, pc0, pc1) in ((jj, 0, na), (jj + 1, na, n)):
                                        if pc1 <= pc0:
                                            continue
                                        b = m // 4
                                        ql0 = 128 * m - 64 + (x0 if m == jj else 0)
                                        oc0 = ql0 + 64 - 512 * b
                                        Ob, ROb = psb[6 + b % 2], R_ps[6 + b % 2]
                                        if m == jj:
                                            stt, stp = (jj == 0), True
                                        else:
                                            stt, stp = True, (jj == J - 1)
                                        MM(Ob[0:65, oc0:oc0 + (pc1 - pc0)], Vaug[:, j, hh, :], PT[:, pc0:pc1], stt, stp,
                                           [RVb, rpt], [ROb])
                                    evs = []
                                    if jj % 4 == 3:
                                        evs.append(jj // 4)
                                    if jj == J - 1:
                                        if J % 4 != 0:
                                            evs.append((J - 1) // 4)
                                        else:
                                            evs.append(J // 4)
                                    for b in evs:
                                        lo = max(0, 512 * b - 64)
                                        hi = min(L, 512 * b + 448)
                                        if hi <= lo:
                                            continue
                                        Ob, ROb = psb[6 + b % 2], R_ps[6 + b % 2]
                                        oc = slice(lo + 64 - 512 * b, hi + 64 - 512 * b)
                                        tk = ss(c + r * lo, hi - lo, r)
                                        if g == 0:
                                            ACT(acc[0:65, tk], Ob[0:65, oc], AF.Copy, [ROb], [R_acc])
                                        else:
                                            TT(acc[0:65, tk], Ob[0:65, oc], acc[0:65, tk], ALU.add, [ROb], [R_acc])
                    for hh in range(2):
                        head = hp * 2 + hh
                        acc, R_acc = accs[hh]
                        for tb in range(NTB):
                            cols = slice(tb * TB, (tb + 1) * TB)
                            RECIP(acc[64:65, cols], acc[64:65, cols], [], [R_acc])
                            ps, rps = PS()
                            MM(ps[0:64, :], ones32[64:65, 0:64], acc[64:65, cols], True, True, [R_acc, R_const], [rps])
                            ot, rot_ = T16()
                            TT(ot[0:64, :], acc[0:64, cols], ps[0:64, :], ALU.mult, [R_acc, rps], [rot_])
                            DMA("sp", attS_d[head][:, cols], ot[0:64, :], C_misc[tb % 2], [rot_], [R_attS[tb]])

                if stop == "attn":
                    raise _Stop()
                DMA("sp", dect[:], dec_d[l:l + 1, :].partition_broadcast(128), C_misc[2], [], [R_dec])
                ACT(lg[:], dect[:, 0, :], AF.Exp, [R_dec], [R_dec], scale=-1.0)
                ACT(lg[:], lg[:], AF.Ln, [], [R_dec], bias=1.0)
                TS(lg[:], lg[:], -1.0, None, ALU.mult, None, [], [R_dec])
                DT = rtab[:, 0:256].rearrange("p (h i) -> p h i", h=2)
                XI = rtab[:, 256:768].rearrange("p (h d i) -> p h d i", h=2, d=2)
                ZE = rtab[:, 768:772]
                CD = rtab[:, 772:776]
                for hp in range(4):
                    wb, rwb = WNEXT()
                    wv = wb[:, 0:8 * 512].rearrange("p (k f) -> p k f", f=512)
                    wgb, rwgb = WNEXT(keep=True)
                    wg = wgb[:, 0:8 * 256].rearrange("p (k f) -> p k f", f=256)
                    for hh in range(2):
                        head = hp * 2 + hh
                        lf = lg[:, head:head + 1]
                        lb = lg[:, 8 + head:9 + head]
                        ef, ref = T32()
                        ACT(ef[:, 0:128], distF, AF.Exp, [R_misc, R_dec], [ref], scale=lf)
                        ACT(ef[:, 128:256], distB, AF.Exp, [R_misc, R_dec], [ref], scale=lb)
                        TT(DT[:, hh, :], ef[:, 0:128], ef[:, 128:256], ALU.add, [ref], [R_rtab])
                        TS(DT[:, hh, :], DT[:, hh, :], 0.125, None, ALU.mult, None, [], [R_rtab])
                        ACT(XI[:, hh, 0, :], xexp[:, 0, :], AF.Exp, [R_misc, R_dec], [R_rtab], scale=lf)
                        ACT(XI[:, hh, 1, :], xexp[:, 1, :], AF.Exp, [R_misc, R_dec], [R_rtab], scale=lb)
                        ACT(ZE[:, hh * 2:hh * 2 + 1], zexp[:, 0:1], AF.Exp, [R_misc, R_dec], [R_rtab], scale=lf)
                        ACT(ZE[:, hh * 2 + 1:hh * 2 + 2], zexp[:, 1:2], AF.Exp, [R_misc, R_dec], [R_rtab], scale=lb)
                        ACT(CD[:, hh * 2:hh * 2 + 1], zexp[:, 2:3], AF.Exp, [R_misc, R_dec], [R_rtab], scale=lf)
                        ACT(CD[:, hh * 2 + 1:hh * 2 + 2], zexp[:, 2:3], AF.Exp, [R_misc, R_dec], [R_rtab], scale=lb)
                    TS(XI[:, :, :, :], XI[:, :, :, :], 0.125, None, ALU.mult, None, [], [R_rtab])
                    if stop == "ret1":
                        raise _Stop()
                    for tb in range(NTB):
                        cols = slice(tb * TB, (tb + 1) * TB)
                        if "norope" not in DBG:
                            DMA("sp", ropeC[:], rope_d[2][:, cols], C_rope[0], [], [R_rope])
                            DMA("sp", ropeS[:], rope_d[3][:, cols], C_rope[1], [], [R_rope])
                        if "ret1b" in DBG:
                            raise _Stop()
                        for wi, (dstT, rdst) in (((1, (Krot, R_A2)), (0, (Qrot, R_A1))) if 'kfirst' in DBG else enumerate(((Qrot, R_A1), (Krot, R_A2)))):
                            if "samebuf" in DBG:
                                if wi == 0:
                                    _sv = dict(cnt)
                                else:
                                    cnt.update(_sv)
                            ps, rps = PS()
                            for kc in range(8):
                                MM(ps[:], wv[:, kc, wi * 128:(wi + 1) * 128], uT[:, kc, cols], kc == 0, kc == 7,
                                   [rwb, R_uT[tb]], [rps])
                            if "x1" in DBG and wi == 1:
                                raise _Stop()
                            qraw, rq = T16()
                            ACT(qraw[:], ps[:], AF.Copy, [rps], [rq])
                            if "x2" in DBG and wi == 1:
                                raise _Stop()
                            ps2, rps2 = PS()
                            MM(ps2[:], permRbf, qraw[:], True, True, [rq, R_misc], [rps2])
                            if "x3" in DBG and wi == 1:
                                raise _Stop()
                            t1, rt1 = T32()
                            TT(t1[:], ps[:], ropeC[:], ALU.mult, [rps, R_rope] + ([rq] if "serial" in DBG else []), [rt1])
                            if "x4" in DBG and wi == 1:
                                raise _Stop()
                            t2, rt2 = T32()
                            TT(t2[:], ps2[:], ropeS[:], ALU.mult, [rps2, R_rope], [rt2])
                            if "x5" in DBG and wi == 1:
                                raise _Stop()
                            TT(dstT[:, cols], t1[:], t2[:], ALU.add, [rt1, rt2], [rdst])
                            if "ret2a" in DBG:
                                raise _Stop()
                        if "ret2b" in DBG:
                            raise _Stop()
                    if stop == "ret2":
                        raise _Stop()
                    Vtm = A8[:, :].bitcast(BF16).rearrange("p (c e) -> p c e", e=256)
                    for ch0 in range(0, 32, 2):
                        ps, rps = PS()
                        for cq in range(2):
                            ch = ch0 + cq
                            for kc in range(8):
                                MM(ps[:, cq * 256:(cq + 1) * 256], uT[:, kc, ch * 128:(ch + 1) * 128], wv[:, kc, 256:512],
                                   kc == 0, kc == 7, [rwb, R_uT[ch // 4]], [rps])
                        ACT(Vtm[:, ch0:ch0 + 2, :], ps[:].rearrange("p (c e) -> p c e", e=256), AF.Copy, [rps], [R_A8])
                    if stop == "ret3":
                        raise _Stop()
                    Kz = [A5[:, 0:4096].rearrange("p (c d) -> p c d", d=128), A6[:, 0:4096].rearrange("p (c d) -> p c d", d=128)]
                    RKz = [R_A5, R_A6]
                    for ch0 in range(0, 32, 4):
                        ps, rps = PS()
                        psv = ps[:, 0:256].bitcast(BF16)
                        for cq in range(4):
                            ch = ch0 + cq
                            TR(psv[:, cq * 128:(cq + 1) * 128], Krot[:, ch * 128:(ch + 1) * 128], identbf,
                               [R_A2, R_misc], [rps])
                        pv = psv.rearrange("p (c d) -> p c d", d=128)
                        for hh in range(2):
                            for di in range(2):
                                ACT(Kz[di][:, ch0:ch0 + 4, hh * 64:(hh + 1) * 64], pv[:, :, hh * 64:(hh + 1) * 64], AF.Copy,
                                    [rps, R_rtab], [RKz[di]], scale=ZE[:, hh * 2 + di:hh * 2 + di + 1])
                    if stop == "ret4":
                        raise _Stop()
                    for hh in range(2):
                        head = hp * 2 + hh
                        rows = slice(hh * 64, hh * 64 + 64)
                        Sst = A7[:, :].bitcast(BF16).rearrange("p (d c e) -> p d c e", d=2, e=128)
                        for di in range(2):
                            order = list(range(0, 31)) if di == 0 else list(range(31, 0, -1))
                            cur = None
                            for k4 in range(0, len(order), 4):
                                grp = order[k4:k4 + 4]
                                ps, rps = PS()
                                for qi, n_ in enumerate(grp):
                                    MM(ps[:, qi * 128:(qi + 1) * 128], Kz[di][:, n_, :], Vtm[:, n_, hh * 128:(hh + 1) * 128],
                                       True, True, [RKz[di], R_A8], [rps])
                                for qi, n_ in enumerate(grp):
                                    nxt = n_ + 1 if di == 0 else n_ - 1
                                    si = (k4 + qi) % 2
                                    if cur is None:
                                        CP(st32[si][rows, :], ps[rows, qi * 128:(qi + 1) * 128], [rps], [R_st32[si]])
                                    else:
                                        STT(st32[si][rows, :], st32[1 - si][rows, :], CD[rows, hh * 2 + di:hh * 2 + di + 1],
                                            ps[rows, qi * 128:(qi + 1) * 128], ALU.mult, ALU.add,
                                            [rps, R_st32[1 - si], R_rtab], [R_st32[si]])
                                    cur = si
                                    ACT(Sst[rows, di, nxt, :], st32[si][rows, :], AF.Copy, [R_st32[si]], [R_A7])
                        if stop == "ret5":
                            raise _Stop()
                        for tb in range(NTB):
                            cols = slice(tb * TB, (tb + 1) * TB)
                            pss, rpss = PS()
                            for cq in range(4):
                                ct = slice(tb * TB + cq * 128, tb * TB + (cq + 1) * 128)
                                MM(pss[:, cq * 128:(cq + 1) * 128], Krot[rows, ct], Qrot[rows, ct], True, True,
                                   [R_A1, R_A2], [rpss])
                            PT, rpt = T16()
                            TT(PT[:].rearrange("p (c i) -> p c i", i=128), pss[:].rearrange("p (c i) -> p c i", i=128),
                               DT[:, hh, :].unsqueeze(1).to_broadcast([128, 4, 128]), ALU.mult, [rpss, R_rtab], [rpt])
                            qx = []
                            for di in range(2):
                                qt, rqt = T16()
                                TT(qt[rows, :].rearrange("p (c i) -> p c i", i=128),
                                   Qrot[rows, cols].rearrange("p (c i) -> p c i", i=128),
                                   XI[rows, hh, di, :].unsqueeze(1).to_broadcast([64, 4, 128]), ALU.mult,
                                   [R_A1, R_rtab], [rqt])
                                qx.append((qt, rqt))
                            psy, rpsy = PS()
                            for cq in range(4):
                                n_ = tb * 4 + cq
                                osl = psy[:, cq * 128:(cq + 1) * 128]
                                terms = [(Vtm[:, n_, hh * 128:(hh + 1) * 128], PT[:, cq * 128:(cq + 1) * 128], [R_A8, rpt])]
                                if n_ > 0:
                                    terms.append((Sst[rows, 0, n_, :], qx[0][0][rows, cq * 128:(cq + 1) * 128], [R_A7, qx[0][1]]))
                                if n_ < 31:
                                    terms.append((Sst[rows, 1, n_, :], qx[1][0][rows, cq * 128:(cq + 1) * 128], [R_A7, qx[1][1]]))
                                for ti, (lt, rh, rd) in enumerate(terms):
                                    MM(osl, lt, rh, ti == 0, ti == len(terms) - 1, rd, [rpsy])
                            ybf, rybf = T16()
                            ACT(ybf[:], psy[:], AF.Copy, [rpsy], [rybf])
                            ysq, rysq = T16()
                            ACT(ysq[:], psy[:], AF.Square, [rpsy], [rysq])
                            psm, rpsm = PS()
                            MM(psm[:], onesgn[:], ybf[:], True, True, [rybf, R_const], [rpsm])
                            psq, rpsq = PS()
                            MM(psq[:], onesgn[:], ysq[:], True, True, [rysq, R_const], [rpsq])
                            m2, rm2 = T32()
                            ACT(m2[:], psm[:], AF.Square, [rpsm], [rm2])
                            var, rvar = T32()
                            TT(var[:], psq[:], m2[:], ALU.subtract, [rpsq, rm2], [rvar])
                            ACT(var[:], var[:], AF.Sqrt, [], [rvar], bias=GN_EPS)
                            RECIP(var[:], var[:], [], [rvar])
                            mean32, rmean = T32()
                            ACT(mean32[:], psm[:], AF.Copy, [rpsm], [rmean])
                            yc, ryc = T32()
                            TT(yc[:], psy[:], mean32[:], ALU.subtract, [rpsy, rmean], [ryc])
                            TT(yc[:], yc[:], var[:], ALU.mult, [rvar], [ryc])
                            psg, rpsg = PS()
                            for kc in range(8):
                                MM(psg[:], wg[:, kc, hh * 128:(hh + 1) * 128], uT[:, kc, cols], kc == 0, kc == 7,
                                   [rwgb, R_uT[tb]], [rpsg])
                            gs, rgs = T32()
                            ACT(gs[:], psg[:], AF.Silu, [rpsg], [rgs])
                            ot, rot_ = T16()
                            TT(ot[:], yc[:], gs[:], ALU.mult, [ryc, rgs], [rot_])
                            DMA("sp", retS_d[head][:, cols], ot[:], C_misc[3 + tb % 2], [rot_], [R_retS[tb]])

                if stop == "ret":
                    raise _Stop()
                u2T = A1[:, :].rearrange("p (k t) -> p k t", t=512)
                mT = A2[:, :].rearrange("p (k t) -> p k t", t=512)
                attblk = A3[:, :].rearrange("p (k t) -> p k t", t=512)
                retblk = A4[:, :].rearrange("p (k t) -> p k t", t=512)
                hid = [A5[:, 0:4096].rearrange("p (k t) -> p k t", t=512), A6[:, 0:4096].rearrange("p (k t) -> p k t", t=512)]
                R_hid = [R_A5, R_A6]
                last = (l == NL - 1)
                for tb in range(NTB):
                    cols = slice(tb * TB, (tb + 1) * TB)
                    DMA("sp", hblk, hT_v[:, :, cols], C_h[0], [R_hT[tb]], [R_A7])
                    DMA("sp", attblk[0:64], attS_d[:, :, cols].rearrange("h d t -> d h t"), C_misc[0], [R_attS[tb]], [R_A3])
                    DMA("sp", retblk, retS_d[:, :, cols].rearrange("h d t -> d h t"), C_misc[3], [R_retS[tb]], [R_A4])
                    for fc in range(8):
                        if True:
                            wm, rwm = WNEXT()
                            vm = wm[:, :].rearrange("p (k f) -> p k f", f=512)
                            psa, rpsa = PS()
                            for kc in range(8):
                                MM(psa[:], vm[:, kc, 0:128], uT[:, kc, cols], kc == 0, kc == 7, [rwm, R_uT[tb]], [rpsa])
                            sa, rsa = T32()
                            ACT(sa[:], psa[:], AF.Sigmoid, [rpsa], [rsa])
                            psb_, rpsb = PS()
                            for kc in range(8):
                                MM(psb_[:], vm[:, kc, 128:256], uT[:, kc, cols], kc == 0, kc == 7, [rwm, R_uT[tb]], [rpsb])
                            sb_, rsb = T32()
                            ACT(sb_[:], psb_[:], AF.Sigmoid, [rpsb], [rsb])
                            pa, rpa = PS()
                            for h in range(8):
                                MM(pa[:], vm[0:64, h, 256:384], attblk[0:64, h, :], h == 0, h == 7, [rwm, R_A3], [rpa])
                            pr, rpr = PS()
                            for h in range(8):
                                MM(pr[:], vm[:, h, 384:512], retblk[:, h, :], h == 0, h == 7, [rwm, R_A4], [rpr])
                            TT(sa[:], sa[:], pa[:], ALU.mult, [rpa], [rsa])
                            TT(sb_[:], sb_[:], pr[:], ALU.mult, [rpr], [rsb])
                            TT(mT[:, fc, :], sa[:], sb_[:], ALU.add, [rsa, rsb], [R_A2])
                    for half in range(2):
                        wo, rwo = WNEXT()
                        vo = wo[:, :].rearrange("p (k f) -> p k f", f=512)
                        for f4 in range(4):
                            fc = half * 4 + f4
                            ps, rps = PS()
                            for kc in range(8):
                                MM(ps[:], vo[:, kc, f4 * 128:(f4 + 1) * 128], mT[:, kc, :], kc == 0, kc == 7, [rwo, R_A2], [rps])
                            ACT(zblk[:, fc, :], ps[:], AF.Copy, [rps], [R_A8])
                    postnorm_add(zfn, [R_A8], l, 1, hfn, [R_A7])
                    prenorm(hfn, [R_A7], l, 2, lambda kc: u2T[:, kc, :], [R_A1])
                    for half in range(2):
                        for q4 in range(4):
                            wu, rwu = WNEXT()
                            vu = wu[:, :].rearrange("p (k f) -> p k f", f=512)
                            for f4 in range(4):
                                hc = q4 * 4 + f4
                                ps, rps = PS()
                                for kc in range(8):
                                    MM(ps[:], vu[:, kc, f4 * 128:(f4 + 1) * 128], u2T[:, kc, :], kc == 0, kc == 7,
                                       [rwu, R_A1], [rps])
                                rl, rrl = T32()
                                ACT(rl[:], ps[:], AF.Relu, [rps], [rrl])
                                TT(hid[hc // 8][:, hc % 8, :], rl[:], rl[:], ALU.mult, [rrl], [R_hid[hc // 8]])
                        for q4 in range(4):
                            wd, rwd = WNEXT()
                            vd = wd[:, :].rearrange("p (k f) -> p k f", f=256)
                            for f2 in range(2):
                                fc = q4 * 2 + f2
                                ps, rps = PS()
                                for hc in range(16):
                                    MM(ps[:], vd[:, hc, f2 * 128:(f2 + 1) * 128], hid[hc // 8][:, hc % 8, :], hc == 0, hc == 15,
                                       [rwd, R_hid[hc // 8]], [rps])
                                if half == 0:
                                    ACT(zblk[:, fc, :], ps[:], AF.Copy, [rps], [R_A8])
                                else:
                                    TT(zblk[:, fc, :], ps[:], zblk[:, fc, :], ALU.add, [rps], [R_A8])
                    postnorm_add(zfn, [R_A8], l, 3, hfn, [R_A7])
                    prenorm(hfn, [R_A7], l, None, lambda kc: u2T[:, kc, :], [R_A1])
                    pf, rpf = T32(); pf2, rpf2 = T32()
                    for i2, (pt_, rp_) in enumerate(((pf, rpf), (pf2, rpf2))):
                        DMA("sp", pt_[:].rearrange("p (t f) -> p t f", f=256),
                            p_d[l][tb * TB + i2 * 256:tb * TB + (i2 + 1) * 256, :].rearrange("(t p) f -> p t f", p=128),
                            C_misc[5], [], [rp_])
                    pTt = []
                    for fk in range(2):
                        ps, rps = PS()
                        for tt in range(4):
                            src_t, rsrc = ((pf, rpf), (pf2, rpf2))[tt // 2]
                            TR(ps[:, tt * 128:(tt + 1) * 128],
                               src_t[:].rearrange("p (t f) -> p t f", f=256)[:, tt % 2, fk * 128:(fk + 1) * 128], ident32,
                               [rsrc, R_misc], [rps])
                        pT_, rpT = T16()
                        ACT(pT_[:], ps[:], AF.Copy, [rps], [rpT])
                        pTt.append((pT_, rpT))
                    wpp, rwpp = WNEXT()
                    vpp = wpp[:, 0:2048].rearrange("p (k f) -> p k f", f=1024)
                    for fc in range(8):
                        ps, rps = PS()
                        for kc in range(2):
                            MM(ps[:], vpp[:, kc, fc * 128:(fc + 1) * 128], pTt[kc][0][:], kc == 0, kc == 1,
                               [rwpp, pTt[kc][1]], [rps])
                        ACT(zblk[:, fc, :], ps[:], AF.Copy, [rps], [R_A8])
                    rs, rrs = rms_rstd(zfn, [R_A8])
                    for half in range(2):
                        wpg, rwpg = WNEXT()
                        vpg = wpg[:, :].rearrange("p (k f) -> p k f", f=512)
                        for f4 in range(4):
                            fc = half * 4 + f4
                            ps, rps = PS()
                            for kc in range(8):
                                MM(ps[:], vpg[:, kc, f4 * 128:(f4 + 1) * 128], u2T[:, kc, :], kc == 0, kc == 7, [rwpg, R_A1], [rps])
                            gsg, rgsg = T32()
                            ACT(gsg[:], ps[:], AF.Sigmoid, [rps], [rgsg])
                            tmp, rt = T32()
                            STT(tmp[:], zblk[:, fc, :], gcol(l, 4, fc), rs[:], ALU.mult, ALU.mult, [R_A8, rrs, R_const], [rt])
                            TT(tmp[:], tmp[:], gsg[:], ALU.mult, [rgsg], [rt])
                            TT(hblk[:, fc, :], hblk[:, fc, :], tmp[:], ALU.add, [rt], [R_A7])
                    if not last:
                        DMA("sp", hT_v[:, :, cols], hblk, C_h[1], [R_A7], [R_hT[tb]])
                        prenorm(hfn, [R_A7], l + 1, 0, lambda kc: uT[:, kc, cols], [R_uT[tb]])
                    else:
                        oblk = A8[:, :].rearrange("p (t f) -> p t f", f=1024)
                        for tt in range(4):
                            for k2 in range(2):
                                ps, rps = PS()
                                for kq in range(4):
                                    kc = k2 * 4 + kq
                                    TR(ps[:, kq * 128:(kq + 1) * 128], hblk[:, kc, tt * 128:(tt + 1) * 128], ident32,
                                       [R_A7, R_misc], [rps])
                                ACT(oblk[:, tt, k2 * 512:(k2 + 1) * 512], ps[:], AF.Copy, [rps], [R_A8])
                        it = DMA("sp", out_d[cols, :].rearrange("(t p) f -> p t f", p=128), oblk, C_out[tb % 4], [R_A8], [])
                        out_dmas.append(it)

        except _Stop:
            pass

        fence = S.add("sp", lambda e: e.nop(), reads=[], writes=[])
        for ch in S.chans:
            if ch.last is not None and ch.last not in fence.deps:
                fence.deps.append(ch.last)

        nep = S.finalize()
        esems = {e: [es.enter_context(nc.semaphore(f"{e}{k}")) for k in range(nep[e])] for e in S.ENGS}
        csems = [es.enter_context(nc.semaphore(f"c{i}")) for i in range(len(S.chans))]
        with nc.Block() as block:
            S.emit(nc, block, esems, csems)
        nc._sched = S
    return nc


FUSED = True
_PROG = {}


def _get_prog(NL):
    if NL not in _PROG:
        _PROG[NL] = build_program(NL)
    return _PROG[NL]


def _gains_layout(gl):
    NL = gl.shape[0]
    return np.ascontiguousarray(gl.reshape(NL, 5, 8, 128).transpose(3, 0, 1, 2).reshape(128, NL * 40))


def kernel(x, p, w_in, w_att_out, w_ret_out, w_out, w_mlp_up, w_mlp_down, w_ple_gate, w_ple_proj,
           ret_decay_logit, norm_mix_pre, norm_mix_post, norm_mlp_pre, norm_mlp_post, norm_ple):
    f = lambda a: np.ascontiguousarray(np.asarray(a, dtype=np.float32))
    x = f(x); p = f(p)
    W = dict(w_in=f(w_in), w_att_out=f(w_att_out), w_ret_out=f(w_ret_out), w_out=f(w_out),
             w_mlp_up=f(w_mlp_up), w_mlp_down=f(w_mlp_down), w_ple_gate=f(w_ple_gate), w_ple_proj=f(w_ple_proj))
    dec = f(ret_decay_logit).reshape(-1, 16)
    gl = np.stack([f(norm_mix_pre), f(norm_mix_post), f(norm_mlp_pre), f(norm_mlp_post), f(norm_ple)], axis=1)
    rope, misc = host_consts()
    B = x.shape[0]
    depth = W["w_in"].shape[0]
    groups = [list(range(depth))] if FUSED else [[l] for l in range(depth)]
    h = x
    for ls in groups:
        nc = _get_prog(len(ls))
        in_maps = []
        for b in range(B):
            m = {k: np.ascontiguousarray(v[ls]) for k, v in W.items()}
            m["x"] = np.ascontiguousarray(h[b])
            m["p"] = np.ascontiguousarray(p[ls, b])
            m["dec"] = np.ascontiguousarray(dec[ls])
            m["gains"] = _gains_layout(gl[ls])
            m["rope"] = rope
            m["misc"] = misc
            in_maps.append(m)
        res = run_bass_kernel_spmd(nc, in_maps, core_ids=list(range(B)))
        h = np.stack([np.asarray(r["out"], dtype=np.float32) for r in res.results], axis=0)
    return h
```

```python
import numpy as np
import concourse.bass as bass
import concourse.mybir as mybir
from concourse.bass_utils import run_bass_kernel_spmd

F32 = mybir.dt.float32
BF16 = mybir.dt.bfloat16
ALU = mybir.AluOpType
AF = mybir.ActivationFunctionType


EPOCH = 30000


class Res:
    __slots__ = ("w", "r", "name", "excl")

    def __init__(self, name="", excl=False):
        self.w = None
        self.r = []
        self.name = name
        self.excl = excl


class Item:
    __slots__ = ("eng", "fn", "deps", "marked", "chan", "didx", "cpos", "val")

    def __init__(self, eng, fn):
        self.eng = eng
        self.fn = fn
        self.deps = []
        self.marked = False
        self.chan = None
        self.didx = 0
        self.cpos = 0
        self.val = 0


class Chan:
    __slots__ = ("n", "last", "sem", "id")

    def __init__(self, i):
        self.n = 0
        self.last = None
        self.sem = None
        self.id = i


class Sched:
    ENGS = ("pe", "act", "dve", "pool", "sp")

    def __init__(self):
        self.streams = {e: [] for e in self.ENGS}
        self.ncomp = {e: 0 for e in self.ENGS}
        self.chans = []

    def chan(self):
        c = Chan(len(self.chans))
        self.chans.append(c)
        return c

    def add(self, eng, fn, reads=(), writes=(), chan=None, chain=True):
        it = Item(eng, fn)
        it.cpos = self.ncomp[eng]
        raw = []
        ex = [r for r in reads if r.excl]
        if ex:
            reads = [r for r in reads if not r.excl]
            writes = list(writes) + [r for r in ex if r not in writes]
        for r in reads:
            if r.w is not None:
                raw.append(r.w)
        for w in writes:
            if w.w is not None:
                raw.append(w.w)
            raw.extend(w.r)
        if chan is not None:
            it.chan = chan
            chan.n += 1
            it.didx = chan.n
            if chan.last is not None and chain:
                raw.append(chan.last)
            chan.last = it
        seen = set()
        for d in raw:
            if id(d) in seen or d is it:
                continue
            seen.add(id(d))
            if d.chan is not None:
                it.deps.append(d)
                continue
            if d.eng == eng:
                if eng == "pe" and chan is None:
                    continue
                if chan is None and (self.ncomp[eng] - d.cpos) > 2:
                    continue
            d.marked = True
            it.deps.append(d)
        for r in reads:
            r.r.append(it)
        for w in writes:
            w.w = it
            w.r = []
        if chan is None:
            self.ncomp[eng] += 1
        self.streams[eng].append(it)
        return it

    def finalize(self):
        self.nmarked = {}
        for e in self.ENGS:
            v = 0
            for it in self.streams[e]:
                if it.chan is None and it.marked:
                    v += 1
                    it.val = v
            self.nmarked[e] = v
        return {e: max(1, -(-self.nmarked[e] // EPOCH)) for e in self.ENGS}

    def emit(self, nc, block, esems, csems):
        self.nwaits = 0
        self.log = {e: [] for e in self.ENGS}
        for c in self.chans:
            c.sem = csems[c.id]
        engobj = {"pe": block.tensor, "act": block.scalar, "dve": block.vector,
                  "pool": block.gpsimd, "sp": block.sync}

        def run(e):
            def body(eng):
                waited = {}
                for it in self.streams[e]:
                    need = {}
                    for d in it.deps:
                        if d.chan is not None:
                            key = ("c", d.chan.id)
                            sem = d.chan.sem
                            val = 16 * d.didx
                        else:
                            ep = (d.val - 1) // EPOCH
                            key = ("e", d.eng, ep)
                            sem = esems[d.eng][ep]
                            val = (d.val - 1) % EPOCH + 1
                        if waited.get(key, 0) < val and need.get(key, (None, 0))[1] < val:
                            need[key] = (sem, val)
                    for key, (sem, val) in need.items():
                        eng.wait_ge(sem, val)
                        waited[key] = val
                        self.nwaits += 1
                        self.log[e].append(f"   wait {key} >= {val}")
                    self.log[e].append(f"{'DMA' if it.chan is not None else 'op'} val={it.val if it.chan is None else ('c%d:%d' % (it.chan.id, 16*it.didx))} marked={it.marked} tag={getattr(it.fn, '_tag', '')}")
                    ins = it.fn(eng)
                    if it.chan is not None:
                        ins.then_inc(it.chan.sem, 16)
                    elif it.marked:
                        ins.then_inc(esems[e][(it.val - 1) // EPOCH], 1)
            engobj[e](body)

        for e in self.ENGS:
            if self.streams[e]:
                run(e)


from contextlib import ExitStack
import ml_dtypes
import os
DBG = os.environ.get("KDBG", "").split(",")

SEQ = 4096
DM = 1024
TB = 512
NTB = SEQ // TB
QA0, KA0, VA0 = 0, 512, 1024
QR0, KR0, VR0, GR0 = 1536, 2048, 2560, 3584
GA0, GB0 = 4608, 5632
EPS = 1e-6
GN_EPS = 1e-5
NRING = 3


def host_consts():
    pos = np.arange(SEQ, dtype=np.float32)
    inv_a = (np.float32(500000.0) ** (-(np.arange(8, dtype=np.float32) * 2.0 / 16.0))).astype(np.float32)
    ang_a = pos[None, :] * inv_a[:, None]
    CA = np.ones((128, SEQ), np.float32)
    SA = np.zeros((128, SEQ), np.float32)
    for hh in range(2):
        b = hh * 64
        CA[b:b + 8] = np.cos(ang_a)
        CA[b + 8:b + 16] = np.cos(ang_a)
        SA[b:b + 8] = -np.sin(ang_a)
        SA[b + 8:b + 16] = np.sin(ang_a)
    inv_r = (np.float32(10000.0) ** (-(np.arange(32, dtype=np.float32) * 2.0 / 64.0))).astype(np.float32)
    ang_r = pos[None, :] * inv_r[:, None]
    CR = np.zeros((128, SEQ), np.float32)
    SR = np.zeros((128, SEQ), np.float32)
    for hh in range(2):
        b = hh * 64
        CR[b:b + 32] = np.cos(ang_r)
        CR[b + 32:b + 64] = np.cos(ang_r)
        SR[b:b + 32] = -np.sin(ang_r)
        SR[b + 32:b + 64] = np.sin(ang_r)
    rope = np.stack([CA, SA, CR, SR]).astype(np.float32)
    permA = np.zeros((128, 128), np.float32)
    permR = np.zeros((128, 128), np.float32)
    for hh in range(2):
        b = hh * 64
        for d in range(8):
            permA[b + d + 8, b + d] = 1.0
            permA[b + d, b + d + 8] = 1.0
        for d in range(32):
            permR[b + d + 32, b + d] = 1.0
            permR[b + d, b + d + 32] = 1.0
    pp = np.arange(128)[:, None]
    xx = np.arange(256)[None, :]
    maskb = np.where((xx >= pp) & (xx <= pp + 128), 0.0, -30000.0).astype(np.float32)
    jj = np.arange(128)[:, None].astype(np.float32)
    ii = np.arange(128)[None, :].astype(np.float32)
    BIG = 2.0e6
    distF = np.where(ii >= jj, ii - jj, BIG).astype(np.float32)
    distB = np.where(jj > ii, jj - ii, BIG).astype(np.float32)
    zexp = np.zeros((128, 4), np.float32)
    zexp[:, 0] = 127.0 - np.arange(128)
    zexp[:, 1] = np.arange(128)
    zexp[:, 2] = 128.0
    xexp = np.zeros((128, 2, 128), np.float32)
    xexp[:, 0, :] = np.arange(128)[None, :] + 1.0
    xexp[:, 1, :] = 128.0 - np.arange(128)[None, :]
    misc = np.concatenate([np.eye(128, dtype=np.float32), permA, permR, maskb, distF, distB,
                           zexp, xexp.reshape(128, 256)], axis=1)
    return rope, np.ascontiguousarray(misc)


MISC_W = 128 * 3 + 256 + 128 * 2 + 4 + 256


class _Stop(Exception):
    pass


def build_program(NL, stop=None):
    nc = bass.Bass("TRN2", target_bir_lowering=False)

    def din(name, shape, dt=F32):
        return nc.dram_tensor(name, shape, dt, kind="ExternalInput").ap()

    x_d = din("x", [SEQ, DM])
    p_d = din("p", [NL, SEQ, 256])
    w_in_d = din("w_in", [NL, DM, 6656])
    w_ao_d = din("w_att_out", [NL, 512, DM])
    w_ro_d = din("w_ret_out", [NL, 1024, DM])
    w_out_d = din("w_out", [NL, DM, DM])
    w_up_d = din("w_mlp_up", [NL, DM, 4096])
    w_dn_d = din("w_mlp_down", [NL, 4096, DM])
    w_pg_d = din("w_ple_gate", [NL, DM, DM])
    w_pp_d = din("w_ple_proj", [NL, 256, DM])
    dec_d = din("dec", [NL, 16])
    gains_d = din("gains", [128, NL * 5 * 8])
    rope_d = din("rope", [4, 128, SEQ])
    misc_d = din("misc", [128, MISC_W])
    out_d = nc.dram_tensor("out", [SEQ, DM], F32, kind="ExternalOutput").ap()
    hT_d = nc.dram_tensor("hT_scr", [DM, SEQ], F32).ap()
    _dk = dict(kind="ExternalOutput") if "dump" in DBG else {}
    attS_d = nc.dram_tensor("att_scr", [8, 64, SEQ], BF16, **_dk).ap()
    retS_d = nc.dram_tensor("ret_scr", [8, 128, SEQ], BF16, **_dk).ap()
    hT_v = hT_d.rearrange("(kc p) t -> p kc t", p=128)
    R_hT = [Res(f"hT{i}") for i in range(NTB)]
    R_attS = [Res(f"attS{i}") for i in range(NTB)]
    R_retS = [Res(f"retS{i}") for i in range(NTB)]

    es = ExitStack()
    with es:
        def sb(name, shp, dt):
            return es.enter_context(nc.sbuf_tensor("sb_" + name, shp, dt))

        def pst(name, shp, dt):
            return es.enter_context(nc.psum_tensor("pp_" + name, shp, dt))

        S = Sched()
        uT = sb("uT", [128, 8, SEQ], BF16)
        R_uT = [Res(f"uT{i}") for i in range(NTB)]
        misc32 = sb("misc32", [128, MISC_W], F32)
        miscbf = sb("miscbf", [128, 640], BF16)
        R_misc = Res("misc")
        gains = sb("gains", [128, NL * 5 * 8], F32)
        onesbf = sb("onesbf", [128, 128], BF16)
        onesgn = sb("onesgn", [128, 128], BF16)
        ones32 = sb("ones32", [128, 64], F32)
        R_const = Res("const")
        ident32 = misc32[:, 0:128]
        o_ = 384 + 256
        distF = misc32[:, o_:o_ + 128]
        distB = misc32[:, o_ + 128:o_ + 256]
        zexp = misc32[:, o_ + 256:o_ + 260]
        xexp = misc32[:, o_ + 260:o_ + 516].rearrange("p (k i) -> p k i", k=2)
        identbf = miscbf[:, 0:128]
        permAbf = miscbf[:, 128:256]
        permRbf = miscbf[:, 256:384]
        maskbf = miscbf[:, 384:640]

        ring = [sb(f"ring{i}", [128, 4096], BF16) for i in range(NRING)]
        R_ring = [Res(f"ring{i}") for i in range(NRING)]
        C_ring = [S.chan() for _ in range(NRING)]

        A1 = sb("A1", [128, 4096], BF16); R_A1 = Res("A1")
        A2 = sb("A2", [128, 4096], BF16); R_A2 = Res("A2")
        A3 = sb("A3", [128, 4096], BF16); R_A3 = Res("A3")
        A4 = sb("A4", [128, 4096], BF16); R_A4 = Res("A4")
        A5 = sb("A5", [128, 4160], BF16); R_A5 = Res("A5")
        A6 = sb("A6", [128, 4160], BF16); R_A6 = Res("A6")
        A7 = sb("A7", [128, 4096], F32); R_A7 = Res("A7")
        A8 = sb("A8", [128, 4096], F32); R_A8 = Res("A8")
        NT32 = 6
        t32 = [sb(f"t32_{i}", [128, 512], F32) for i in range(NT32)]
        R_t32 = [Res(f"t32_{i}") for i in range(NT32)]
        NTB16 = 6
        t16 = [sb(f"t16_{i}", [128, 512], BF16) for i in range(NTB16)]
        R_t16 = [Res(f"t16_{i}") for i in range(NTB16)]
        ropeC = sb("ropeC", [128, 512], F32); ropeS = sb("ropeS", [128, 512], F32)
        R_rope = Res("rope")
        C_rope = [S.chan(), S.chan()]
        dect = sb("dect", [128, 1, 16], F32)
        lg = sb("lg", [128, 16], F32)
        rtab = sb("rtab", [128, 2 * 128 + 4 * 128 + 8], F32)
        R_dec = Res("dec"); R_rtab = Res("rtab")
        st32 = [sb(f"st32_{i}", [128, 128], F32) for i in range(2)]
        R_st32 = [Res("st32a"), Res("st32b")]

        rsb_ = [sb(f"rs_{i}", [128, 512], F32) for i in range(2)]
        R_rsb = [Res("rs0"), Res("rs1")]
        cnt = {"t32": 0, "t16": 0, "ps": 0, "rs": 0}
        if "verbose" in DBG:
            print("SBUF bytes remaining after alloc:", nc.sbuf_bytes_remaining)

        def T32():
            i = cnt["t32"] % NT32; cnt["t32"] += 1
            return t32[i], R_t32[i]

        def T16():
            i = cnt["t16"] % NTB16; cnt["t16"] += 1
            return t16[i], R_t16[i]

        psb = [pst(f"ps{i}", [128, 512], F32) for i in range(8)]
        R_ps = [Res(f"ps{i}", excl=True) for i in range(8)]

        def PS():
            i = cnt["ps"] % 6; cnt["ps"] += 1
            return psb[i], R_ps[i]

        def MM(out, lhsT, rhs, start, stop, rd, wr):
            return S.add("pe", lambda e: e.matmul(out, lhsT, rhs, start=start, stop=stop), reads=rd, writes=wr)

        def TR(out, in_, ident, rd, wr):
            return S.add("pe", lambda e: e.transpose(out, in_, ident), reads=rd, writes=wr)

        def ACT(out, in_, func, rd, wr, scale=1.0, bias=0.0):
            return S.add("act", lambda e: e.activation(out, in_, func, bias=bias, scale=scale), reads=rd, writes=wr)

        def TT(out, a, b, op, rd, wr, eng="dve"):
            return S.add(eng, lambda e: e.tensor_tensor(out, a, b, op), reads=rd, writes=wr)

        def STT(out, in0, scalar, in1, op0, op1, rd, wr, eng="dve"):
            return S.add(eng, lambda e: e.scalar_tensor_tensor(out, in0, scalar, in1, op0, op1), reads=rd, writes=wr)

        def TS(out, in0, s1, s2, op0, op1, rd, wr, eng="dve"):
            if s2 is None:
                return S.add(eng, lambda e: e.tensor_scalar(out, in0, s1, None, op0), reads=rd, writes=wr)
            return S.add(eng, lambda e: e.tensor_scalar(out, in0, s1, s2, op0, op1), reads=rd, writes=wr)

        def CP(out, in_, rd, wr, eng="dve"):
            return S.add(eng, lambda e: e.tensor_copy(out, in_), reads=rd, writes=wr)

        def RECIP(out, in_, rd, wr):
            return S.add("dve", lambda e: e.reciprocal(out, in_), reads=rd, writes=wr)

        def MEMSET(ap, val, wr, eng="dve"):
            return S.add(eng, lambda e: e.memset(ap, val), writes=wr)

        def DMA(q, out, in_, chan, rd, wr):
            return S.add(q, lambda e: e.dma_start(out=out, in_=in_), reads=rd, writes=wr, chan=chan)

        out_dmas = []

        def ss(start, count, step):
            return slice(start, start + step * (count - 1) + 1, step)

        wq = {"specs": [], "issued": 0, "used": 0, "base": 0}

        def w_issue_upto(k):
            while wq["issued"] < min(k, len(wq["specs"])):
                i = wq["issued"]
                b = i % NRING
                for di_, (dst_fn, src) in enumerate(wq["specs"][i]):
                    if di_ == 0:
                        DMA("pool", dst_fn(ring[b]), src, C_ring[b], [], [R_ring[b]])
                    else:
                        it_ = S.add("pool", (lambda o_, i_: (lambda e: e.dma_start(out=o_, in_=i_)))(dst_fn(ring[b]), src),
                                    reads=[], writes=[], chan=C_ring[b], chain=False)
                        R_ring[b].w = it_
                wq["issued"] += 1

        def WNEXT(keep=False):
            i = wq["used"]
            if not keep:
                wq["base"] = i
            w_issue_upto(wq["base"] + NRING)
            wq["used"] += 1
            b = i % NRING
            return ring[b], R_ring[b]

        def panel(src_ap, kcn, ncols, col0=0, width=None, parts=128):
            width = width or ncols

            def dst(rb):
                return rb[0:parts, 0:kcn * width].rearrange("p (k f) -> p k f", f=width)[:, :, col0:col0 + ncols]
            return (dst, src_ap)

        def attn_specs(l, hp):
            wv = w_in_d[l].rearrange("(kc p) f -> p kc f", p=128)
            return [[panel(wv[:, :, QA0 + hp * 128:QA0 + hp * 128 + 128], 8, 128, 0, 384),
                     panel(wv[:, :, KA0 + hp * 128:KA0 + hp * 128 + 128], 8, 128, 128, 384),
                     panel(wv[:, :, VA0 + hp * 128:VA0 + hp * 128 + 128], 8, 128, 256, 384)]]

        def ret_specs(l, hp):
            wv = w_in_d[l].rearrange("(kc p) f -> p kc f", p=128)
            return [[panel(wv[:, :, QR0 + hp * 128:QR0 + hp * 128 + 128], 8, 128, 0, 512),
                     panel(wv[:, :, KR0 + hp * 128:KR0 + hp * 128 + 128], 8, 128, 128, 512),
                     panel(wv[:, :, VR0 + hp * 256:VR0 + hp * 256 + 256], 8, 256, 256, 512)],
                    [panel(wv[:, :, GR0 + hp * 256:GR0 + hp * 256 + 256], 8, 256, 0, 256)]]

        def phc_specs(l):
            wv = w_in_d[l].rearrange("(kc p) f -> p kc f", p=128)
            sp = []
            wao = w_ao_d[l].rearrange("(h p) f -> p h f", p=64)
            wro = w_ro_d[l].rearrange("(kc p) f -> p kc f", p=128)
            for fc in range(8):
                c0 = fc * 128
                sp.append([panel(wv[:, :, GA0 + c0:GA0 + c0 + 128], 8, 128, 0, 512),
                           panel(wv[:, :, GB0 + c0:GB0 + c0 + 128], 8, 128, 128, 512),
                           panel(wao[:, :, c0:c0 + 128], 8, 128, 256, 512, parts=64),
                           panel(wro[:, :, c0:c0 + 128], 8, 128, 384, 512)])
            def split2(src, kcn, ncols):
                h_ = kcn // 2

                def mk(k0):
                    def dst(rb):
                        return rb[:, 0:kcn * ncols].rearrange("p (k f) -> p k f", f=ncols)[:, k0:k0 + h_, :]
                    return dst
                return [(mk(0), src[:, 0:h_, :]), (mk(h_), src[:, h_:kcn, :])]
            wo = w_out_d[l].rearrange("(kc p) f -> p kc f", p=128)
            for half in range(2):
                sp.append(split2(wo[:, :, half * 512:half * 512 + 512], 8, 512))
            wu = w_up_d[l].rearrange("(kc p) f -> p kc f", p=128)
            wd = w_dn_d[l].rearrange("(kc p) f -> p kc f", p=128)
            for half in range(2):
                for q4 in range(4):
                    c0 = half * 2048 + q4 * 512
                    sp.append(split2(wu[:, :, c0:c0 + 512], 8, 512))
                for q4 in range(4):
                    sp.append(split2(wd[:, half * 16:half * 16 + 16, q4 * 256:q4 * 256 + 256], 16, 256))
            wpp = w_pp_d[l].rearrange("(kc p) f -> p kc f", p=128)
            sp.append([panel(wpp[:, :, :], 2, 1024)])
            wpg = w_pg_d[l].rearrange("(kc p) f -> p kc f", p=128)
            for half in range(2):
                sp.append(split2(wpg[:, :, half * 512:half * 512 + 512], 8, 512))
            return sp

        for l in range(NL):
            for hp in range(4):
                wq["specs"] += attn_specs(l, hp)
            for hp in range(4):
                wq["specs"] += ret_specs(l, hp)
            for tb in range(NTB):
                wq["specs"] += phc_specs(l)

        C_c = [S.chan() for _ in range(4)]
        DMA("sp", misc32[:], misc_d, C_c[0], [], [R_misc])
        DMA("sp", gains[:], gains_d, C_c[1], [], [R_const])
        DMA("pool", miscbf[:], misc_d[:, 0:640], C_c[2], [], [R_misc])
        MEMSET(onesbf[:], 1.0, [R_const])
        MEMSET(onesgn[:], 1.0 / 128.0, [R_const])
        MEMSET(ones32[:], 1.0, [R_const])

        def gcol(l, which, kc):
            c = (l * 5 + which) * 8 + kc
            return gains[:, c:c + 1]

        def rms_rstd(src_fn, src_res, n_feat_chunks=8, inv_n=1.0 / 1024.0):
            ps, rps = PS()
            for kc in range(n_feat_chunks):
                sq, rsq = T16()
                ACT(sq[:], src_fn(kc), AF.Square, src_res, [rsq])
                MM(ps[:], onesbf[:], sq[:], kc == 0, kc == n_feat_chunks - 1, [rsq, R_const], [rps])
            i_ = cnt["rs"] % 2; cnt["rs"] += 1
            rs, rrs = rsb_[i_], R_rsb[i_]
            ACT(rs[:], ps[:], AF.Sqrt, [rps], [rrs], scale=inv_n, bias=EPS)
            RECIP(rs[:], rs[:], [], [rrs])
            return rs, rrs

        def prenorm(hblk_fn, hres, l, which, dst_fn, dres):
            rs, rrs = rms_rstd(hblk_fn, hres)
            for kc in range(8):
                if which is None:
                    TT(dst_fn(kc), hblk_fn(kc), rs[:], ALU.mult, hres + [rrs], dres)
                else:
                    STT(dst_fn(kc), hblk_fn(kc), gcol(l, which, kc), rs[:], ALU.mult, ALU.mult,
                        hres + [rrs, R_const], dres)

        def postnorm_add(z_fn, zres, l, which, h_fn, hres):
            rs, rrs = rms_rstd(z_fn, zres)
            for kc in range(8):
                tmp, rt = T32()
                STT(tmp[:], z_fn(kc), gcol(l, which, kc), rs[:], ALU.mult, ALU.mult, zres + [rrs, R_const], [rt])
                TT(h_fn(kc), h_fn(kc), tmp[:], ALU.add, [rt], hres)

        hblk = A7[:, :].rearrange("p (k t) -> p k t", t=512)
        zblk = A8[:, :].rearrange("p (k t) -> p k t", t=512)

        def hfn(kc):
            return hblk[:, kc, :]

        def zfn(kc):
            return zblk[:, kc, :]

        C_h = [S.chan(), S.chan()]
        C_misc = [S.chan() for _ in range(6)]
        C_out = [S.chan() for _ in range(4)]

        try:
            xblk = A8[:, :].rearrange("p (t f) -> p t f", f=1024)
            for tb in range(NTB):
                cols = slice(tb * TB, (tb + 1) * TB)
                DMA("sp", xblk, x_d[cols, :].rearrange("(t p) f -> p t f", p=128), C_h[0], [], [R_A8])
                for kc in range(8):
                    ps, rps = PS()
                    for tt in range(4):
                        TR(ps[:, tt * 128:(tt + 1) * 128], xblk[:, tt, kc * 128:(kc + 1) * 128], ident32,
                           [R_A8, R_misc], [rps])
                    ACT(hblk[:, kc, :], ps[:], AF.Copy, [rps], [R_A7])
                DMA("sp", hT_v[:, :, cols], hblk, C_h[1], [R_A7], [R_hT[tb]])
                prenorm(hfn, [R_A7], 0, 0, lambda kc: uT[:, kc, cols], [R_uT[tb]])

            if stop == "p0":
                raise _Stop()
            Qrot, Krot, VT = A1, A2, A3
            for l in range(NL):
                for hp in range(4):
                    wb, rwb = WNEXT()
                    if "noattn" in DBG:
                        continue
                    wv = wb[:, 0:8 * 384].rearrange("p (k f) -> p k f", f=384)
                    for tb in range(NTB):
                        cols = slice(tb * TB, (tb + 1) * TB)
                        DMA("sp", ropeC[:], rope_d[0][:, cols], C_rope[0], [], [R_rope])
                        DMA("sp", ropeS[:], rope_d[1][:, cols], C_rope[1], [], [R_rope])
                        for wi, (dstT, rdst) in enumerate(((Qrot, R_A1), (Krot, R_A2))):
                            ps, rps = PS()
                            for kc in range(8):
                                MM(ps[:], wv[:, kc, wi * 128:(wi + 1) * 128], uT[:, kc, cols], kc == 0, kc == 7,
                                   [rwb, R_uT[tb]], [rps])
                            qraw, rq = T16()
                            ACT(qraw[:], ps[:], AF.Copy, [rps], [rq])
                            ps2, rps2 = PS()
                            MM(ps2[:], permAbf, qraw[:], True, True, [rq, R_misc], [rps2])
                            t1, rt1 = T32()
                            TT(t1[:], ps[:], ropeC[:], ALU.mult, [rps, R_rope], [rt1])
                            t2, rt2 = T32()
                            TT(t2[:], ps2[:], ropeS[:], ALU.mult, [rps2, R_rope], [rt2])
                            TT(dstT[:, cols], t1[:], t2[:], ALU.add, [rt1, rt2], [rdst])
                        ps, rps = PS()
                        for kc in range(8):
                            MM(ps[:], wv[:, kc, 256:384], uT[:, kc, cols], kc == 0, kc == 7, [rwb, R_uT[tb]], [rps])
                        ACT(VT[:, cols], ps[:], AF.Copy, [rps], [R_A3])
                    accs = ((A7, R_A7), (A8, R_A8))
                    for g, r in enumerate((1, 4, 16)):
                        for hh in range(2):
                            head = hp * 2 + hh
                            rows = slice(hh * 64, hh * 64 + 64)
                            acc, R_acc = accs[hh]
                            Vb, RVb = (A5, R_A5) if g % 2 == 0 else (A6, R_A6)
                            Vaug = Vb[:, 0:4160].rearrange("p (c h d) -> p c h d", h=2, d=65)
                            J = 32 // r
                            L = SEQ // r
                            if hh == 0:
                                MEMSET(Vaug[:, :, :, 64:65], 1.0, [RVb])
                                for j0 in range(0, 32, 4):
                                    ps, rps = PS()
                                    psv = ps[:, 0:256].bitcast(BF16)
                                    for jq in range(4):
                                        j = j0 + jq
                                        c, jj = divmod(j, J)
                                        t0 = c + r * 128 * jj
                                        TR(psv[:, jq * 128:(jq + 1) * 128], VT[:, ss(t0, 128, r)], identbf,
                                           [R_A3, R_misc], [rps])
                                    ACT(Vaug[:, j0:j0 + 4, :, 0:64],
                                        psv.rearrange("p (c h d) -> p c h d", h=2, d=64), AF.Copy, [rps], [RVb])
                            pend = None
                            for c in range(r):
                                for jj in range(J):
                                    j = c * J + jj
                                    x0 = 64 if jj == 0 else 0
                                    x1 = 192 if jj == J - 1 else 256
                                    n = x1 - x0
                                    l0 = 128 * jj - 64 + x0
                                    kt0 = c + r * 128 * jj
                                    ktok = ss(kt0, 128, r)
                                    qt0 = c + r * l0
                                    qtok = ss(qt0, n, r)
                                    st, rst = PS()
                                    MM(st[:, 0:n], Krot[rows, ktok], Qrot[rows, qtok], True, False, [R_A1, R_A2], [rst])
                                    MM(st[:, 0:n], identbf, maskbf[:, x0:x1], False, True, [R_misc], [rst])
                                    PT, rpt = T16()
                                    ACT(PT[:, 0:n], st[:, 0:n], AF.Exp, [rst], [rpt], scale=0.125)
                                    def _pv(j=j, jj=jj, x0=x0, n=n, PT=PT, rpt=rpt, c=c):
                                        na = 128 - x0
                                        for (m, pc0, pc1) in ((jj, 0, na), (jj + 1, na, n)):
                                            if pc1 <= pc0:
                                                continue
                                            b = m // 4
                                            ql0 = 128 * m - 64 + (x0 if m == jj else 0)
                                            oc0 = ql0 + 64 - 512 * b
                                            Ob, ROb = psb[6 + b % 2], R_ps[6 + b % 2]
                                            if m == jj:
                                                stt, stp = (jj == 0), True
                                            else:
                                                stt, stp = True, (jj == J - 1)
                                            MM(Ob[0:65, oc0:oc0 + (pc1 - pc0)], Vaug[:, j, hh, :], PT[:, pc0:pc1], stt, stp,
                                               [RVb, rpt], [ROb])
                                        evs = []
                                        if jj % 4 == 3:
                                            evs.append(jj // 4)
                                        if jj == J - 1:
                                            if J % 4 != 0:
                                                evs.append((J - 1) // 4)
                                            else:
                                                evs.append(J // 4)
                                        for b in evs:
                                            lo = max(0, 512 * b - 64)
                                            hi = min(L, 512 * b + 448)
                                            if hi <= lo:
                                                continue
                                            Ob, ROb = psb[6 + b % 2], R_ps[6 + b % 2]
                                            oc = slice(lo + 64 - 512 * b, hi + 64 - 512 * b)
                                            tk = ss(c + r * lo, hi - lo, r)
                                            if g == 0:
                                                ACT(acc[0:65, tk], Ob[0:65, oc], AF.Copy, [ROb], [R_acc])
                                            else:
                                                TT(acc[0:65, tk], Ob[0:65, oc], acc[0:65, tk], ALU.add, [ROb], [R_acc])
                                    if pend is not None:
                                        pend()
                                    pend = _pv
                            if pend is not None:
                                pend()
                                pend = None
                    for hh in range(2):
                        head = hp * 2 + hh
                        acc, R_acc = accs[hh]
                        for tb in range(NTB):
                            cols = slice(tb * TB, (tb + 1) * TB)
                            RECIP(acc[64:65, cols], acc[64:65, cols], [], [R_acc])
                            ps, rps = PS()
                            MM(ps[0:64, :], ones32[64:65, 0:64], acc[64:65, cols], True, True, [R_acc, R_const], [rps])
                            ot, rot_ = T16()
                            TT(ot[0:64, :], acc[0:64, cols], ps[0:64, :], ALU.mult, [R_acc, rps], [rot_])
                            DMA("sp", attS_d[head][:, cols], ot[0:64, :], C_misc[tb % 2], [rot_], [R_attS[tb]])

                if stop == "attn":
                    raise _Stop()
                DMA("sp", dect[:], dec_d[l:l + 1, :].partition_broadcast(128), C_misc[2], [], [R_dec])
                ACT(lg[:], dect[:, 0, :], AF.Exp, [R_dec], [R_dec], scale=-1.0)
                ACT(lg[:], lg[:], AF.Ln, [], [R_dec], bias=1.0)
                TS(lg[:], lg[:], -1.0, None, ALU.mult, None, [], [R_dec])
                DT = rtab[:, 0:256].rearrange("p (h i) -> p h i", h=2)
                XI = rtab[:, 256:768].rearrange("p (h d i) -> p h d i", h=2, d=2)
                ZE = rtab[:, 768:772]
                CD = rtab[:, 772:776]
                for hp in range(4):
                    wb, rwb = WNEXT()
                    wv = wb[:, 0:8 * 512].rearrange("p (k f) -> p k f", f=512)
                    wgb, rwgb = WNEXT(keep=True)
                    wg = wgb[:, 0:8 * 256].rearrange("p (k f) -> p k f", f=256)
                    for hh in range(2):
                        head = hp * 2 + hh
                        lf = lg[:, head:head + 1]
                        lb = lg[:, 8 + head:9 + head]
                        ef, ref = T32()
                        ACT(ef[:, 0:128], distF, AF.Exp, [R_misc, R_dec], [ref], scale=lf)
                        ACT(ef[:, 128:256], distB, AF.Exp, [R_misc, R_dec], [ref], scale=lb)
                        TT(DT[:, hh, :], ef[:, 0:128], ef[:, 128:256], ALU.add, [ref], [R_rtab])
                        TS(DT[:, hh, :], DT[:, hh, :], 0.125, None, ALU.mult, None, [], [R_rtab])
                        ACT(XI[:, hh, 0, :], xexp[:, 0, :], AF.Exp, [R_misc, R_dec], [R_rtab], scale=lf)
                        ACT(XI[:, hh, 1, :], xexp[:, 1, :], AF.Exp, [R_misc, R_dec], [R_rtab], scale=lb)
                        ACT(ZE[:, hh * 2:hh * 2 + 1], zexp[:, 0:1], AF.Exp, [R_misc, R_dec], [R_rtab], scale=lf)
                        ACT(ZE[:, hh * 2 + 1:hh * 2 + 2], zexp[:, 1:2], AF.Exp, [R_misc, R_dec], [R_rtab], scale=lb)
                        ACT(CD[:, hh * 2:hh * 2 + 1], zexp[:, 2:3], AF.Exp, [R_misc, R_dec], [R_rtab], scale=lf)
                        ACT(CD[:, hh * 2 + 1:hh * 2 + 2], zexp[:, 2:3], AF.Exp, [R_misc, R_dec], [R_rtab], scale=lb)
                    TS(XI[:, :, :, :], XI[:, :, :, :], 0.125, None, ALU.mult, None, [], [R_rtab])
                    if stop == "ret1":
                        raise _Stop()
                    for tb in range(NTB):
                        cols = slice(tb * TB, (tb + 1) * TB)
                        if "norope" not in DBG:
                            DMA("sp", ropeC[:], rope_d[2][:, cols], C_rope[0], [], [R_rope])
                            DMA("sp", ropeS[:], rope_d[3][:, cols], C_rope[1], [], [R_rope])
                        if "ret1b" in DBG:
                            raise _Stop()
                        for wi, (dstT, rdst) in (((1, (Krot, R_A2)), (0, (Qrot, R_A1))) if 'kfirst' in DBG else enumerate(((Qrot, R_A1), (Krot, R_A2)))):
                            if "samebuf" in DBG:
                                if wi == 0:
                                    _sv = dict(cnt)
                                else:
                                    cnt.update(_sv)
                            ps, rps = PS()
                            for kc in range(8):
                                MM(ps[:], wv[:, kc, wi * 128:(wi + 1) * 128], uT[:, kc, cols], kc == 0, kc == 7,
                                   [rwb, R_uT[tb]], [rps])
                            if "x1" in DBG and wi == 1:
                                raise _Stop()
                            qraw, rq = T16()
                            ACT(qraw[:], ps[:], AF.Copy, [rps], [rq])
                            if "x2" in DBG and wi == 1:
                                raise _Stop()
                            ps2, rps2 = PS()
                            MM(ps2[:], permRbf, qraw[:], True, True, [rq, R_misc], [rps2])
                            if "x3" in DBG and wi == 1:
                                raise _Stop()
                            t1, rt1 = T32()
                            TT(t1[:], ps[:], ropeC[:], ALU.mult, [rps, R_rope] + ([rq] if "serial" in DBG else []), [rt1])
                            if "x4" in DBG and wi == 1:
                                raise _Stop()
                            t2, rt2 = T32()
                            TT(t2[:], ps2[:], ropeS[:], ALU.mult, [rps2, R_rope], [rt2])
                            if "x5" in DBG and wi == 1:
                                raise _Stop()
                            TT(dstT[:, cols], t1[:], t2[:], ALU.add, [rt1, rt2], [rdst])
                            if "ret2a" in DBG:
                                raise _Stop()
                        if "ret2b" in DBG:
                            raise _Stop()
                    if stop == "ret2":
                        raise _Stop()
                    Vtm = A8[:, :].bitcast(BF16).rearrange("p (c e) -> p c e", e=256)
                    for ch0 in range(0, 32, 2):
                        ps, rps = PS()
                        for cq in range(2):
                            ch = ch0 + cq
                            for kc in range(8):
                                MM(ps[:, cq * 256:(cq + 1) * 256], uT[:, kc, ch * 128:(ch + 1) * 128], wv[:, kc, 256:512],
                                   kc == 0, kc == 7, [rwb, R_uT[ch // 4]], [rps])
                        ACT(Vtm[:, ch0:ch0 + 2, :], ps[:].rearrange("p (c e) -> p c e", e=256), AF.Copy, [rps], [R_A8])
                    if stop == "ret3":
                        raise _Stop()
                    Kz = [A5[:, 0:4096].rearrange("p (c d) -> p c d", d=128), A6[:, 0:4096].rearrange("p (c d) -> p c d", d=128)]
                    RKz = [R_A5, R_A6]
                    for ch0 in range(0, 32, 4):
                        ps, rps = PS()
                        psv = ps[:, 0:256].bitcast(BF16)
                        for cq in range(4):
                            ch = ch0 + cq
                            TR(psv[:, cq * 128:(cq + 1) * 128], Krot[:, ch * 128:(ch + 1) * 128], identbf,
                               [R_A2, R_misc], [rps])
                        pv = psv.rearrange("p (c d) -> p c d", d=128)
                        for hh in range(2):
                            for di in range(2):
                                ACT(Kz[di][:, ch0:ch0 + 4, hh * 64:(hh + 1) * 64], pv[:, :, hh * 64:(hh + 1) * 64], AF.Copy,
                                    [rps, R_rtab], [RKz[di]], scale=ZE[:, hh * 2 + di:hh * 2 + di + 1])
                    if stop == "ret4":
                        raise _Stop()
                    for hh in range(2):
                        head = hp * 2 + hh
                        rows = slice(hh * 64, hh * 64 + 64)
                        Sst = A7[:, :].bitcast(BF16).rearrange("p (d c e) -> p d c e", d=2, e=128)
                        for di in range(2):
                            order = list(range(0, 31)) if di == 0 else list(range(31, 0, -1))
                            cur = None
                            for k4 in range(0, len(order), 4):
                                grp = order[k4:k4 + 4]
                                ps, rps = PS()
                                for qi, n_ in enumerate(grp):
                                    MM(ps[:, qi * 128:(qi + 1) * 128], Kz[di][:, n_, :], Vtm[:, n_, hh * 128:(hh + 1) * 128],
                                       True, True, [RKz[di], R_A8], [rps])
                                for qi, n_ in enumerate(grp):
                                    nxt = n_ + 1 if di == 0 else n_ - 1
                                    si = (k4 + qi) % 2
                                    if cur is None:
                                        CP(st32[si][rows, :], ps[rows, qi * 128:(qi + 1) * 128], [rps], [R_st32[si]])
                                    else:
                                        STT(st32[si][rows, :], st32[1 - si][rows, :], CD[rows, hh * 2 + di:hh * 2 + di + 1],
                                            ps[rows, qi * 128:(qi + 1) * 128], ALU.mult, ALU.add,
                                            [rps, R_st32[1 - si], R_rtab], [R_st32[si]])
                                    cur = si
                                    ACT(Sst[rows, di, nxt, :], st32[si][rows, :], AF.Copy, [R_st32[si]], [R_A7])
                        if stop == "ret5":
                            raise _Stop()
                        for tb in range(NTB):
                            cols = slice(tb * TB, (tb + 1) * TB)
                            pss, rpss = PS()
                            for cq in range(4):
                                ct = slice(tb * TB + cq * 128, tb * TB + (cq + 1) * 128)
                                MM(pss[:, cq * 128:(cq + 1) * 128], Krot[rows, ct], Qrot[rows, ct], True, True,
                                   [R_A1, R_A2], [rpss])
                            PT, rpt = T16()
                            TT(PT[:].rearrange("p (c i) -> p c i", i=128), pss[:].rearrange("p (c i) -> p c i", i=128),
                               DT[:, hh, :].unsqueeze(1).to_broadcast([128, 4, 128]), ALU.mult, [rpss, R_rtab], [rpt])
                            qx = []
                            for di in range(2):
                                qt, rqt = T16()
                                TT(qt[rows, :].rearrange("p (c i) -> p c i", i=128),
                                   Qrot[rows, cols].rearrange("p (c i) -> p c i", i=128),
                                   XI[rows, hh, di, :].unsqueeze(1).to_broadcast([64, 4, 128]), ALU.mult,
                                   [R_A1, R_rtab], [rqt])
                                qx.append((qt, rqt))
                            psy, rpsy = PS()
                            for cq in range(4):
                                n_ = tb * 4 + cq
                                osl = psy[:, cq * 128:(cq + 1) * 128]
                                terms = [(Vtm[:, n_, hh * 128:(hh + 1) * 128], PT[:, cq * 128:(cq + 1) * 128], [R_A8, rpt])]
                                if n_ > 0:
                                    terms.append((Sst[rows, 0, n_, :], qx[0][0][rows, cq * 128:(cq + 1) * 128], [R_A7, qx[0][1]]))
                                if n_ < 31:
                                    terms.append((Sst[rows, 1, n_, :], qx[1][0][rows, cq * 128:(cq + 1) * 128], [R_A7, qx[1][1]]))
                                for ti, (lt, rh, rd) in enumerate(terms):
                                    MM(osl, lt, rh, ti == 0, ti == len(terms) - 1, rd, [rpsy])
                            ybf, rybf = T16()
                            ACT(ybf[:], psy[:], AF.Copy, [rpsy], [rybf])
                            ysq, rysq = T16()
                            ACT(ysq[:], psy[:], AF.Square, [rpsy], [rysq])
                            psm, rpsm = PS()
                            MM(psm[:], onesgn[:], ybf[:], True, True, [rybf, R_const], [rpsm])
                            psq, rpsq = PS()
                            MM(psq[:], onesgn[:], ysq[:], True, True, [rysq, R_const], [rpsq])
                            m2, rm2 = T32()
                            ACT(m2[:], psm[:], AF.Square, [rpsm], [rm2])
                            var, rvar = T32()
                            TT(var[:], psq[:], m2[:], ALU.subtract, [rpsq, rm2], [rvar])
                            ACT(var[:], var[:], AF.Sqrt, [], [rvar], bias=GN_EPS)
                            RECIP(var[:], var[:], [], [rvar])
                            mean32, rmean = T32()
                            ACT(mean32[:], psm[:], AF.Copy, [rpsm], [rmean])
                            yc, ryc = T32()
                            TT(yc[:], psy[:], mean32[:], ALU.subtract, [rpsy, rmean], [ryc])
                            TT(yc[:], yc[:], var[:], ALU.mult, [rvar], [ryc])
                            psg, rpsg = PS()
                            for kc in range(8):
                                MM(psg[:], wg[:, kc, hh * 128:(hh + 1) * 128], uT[:, kc, cols], kc == 0, kc == 7,
                                   [rwgb, R_uT[tb]], [rpsg])
                            gs, rgs = T32()
                            ACT(gs[:], psg[:], AF.Silu, [rpsg], [rgs])
                            ot, rot_ = T16()
                            TT(ot[:], yc[:], gs[:], ALU.mult, [ryc, rgs], [rot_])
                            DMA("sp", retS_d[head][:, cols], ot[:], C_misc[3 + tb % 2], [rot_], [R_retS[tb]])

                if stop == "ret":
                    raise _Stop()
                u2T = A1[:, :].rearrange("p (k t) -> p k t", t=512)
                mT = A2[:, :].rearrange("p (k t) -> p k t", t=512)
                attblk = A3[:, :].rearrange("p (k t) -> p k t", t=512)
                retblk = A4[:, :].rearrange("p (k t) -> p k t", t=512)
                hid = [A5[:, 0:4096].rearrange("p (k t) -> p k t", t=512), A6[:, 0:4096].rearrange("p (k t) -> p k t", t=512)]
                R_hid = [R_A5, R_A6]
                last = (l == NL - 1)
                for tb in range(NTB):
                    cols = slice(tb * TB, (tb + 1) * TB)
                    DMA("sp", hblk, hT_v[:, :, cols], C_h[0], [R_hT[tb]], [R_A7])
                    DMA("sp", attblk[0:64], attS_d[:, :, cols].rearrange("h d t -> d h t"), C_misc[0], [R_attS[tb]], [R_A3])
                    DMA("sp", retblk, retS_d[:, :, cols].rearrange("h d t -> d h t"), C_misc[3], [R_retS[tb]], [R_A4])
                    for fc in range(8):
                        if True:
                            wm, rwm = WNEXT()
                            vm = wm[:, :].rearrange("p (k f) -> p k f", f=512)
                            psa, rpsa = PS()
                            for kc in range(8):
                                MM(psa[:], vm[:, kc, 0:128], uT[:, kc, cols], kc == 0, kc == 7, [rwm, R_uT[tb]], [rpsa])
                            sa, rsa = T32()
                            ACT(sa[:], psa[:], AF.Sigmoid, [rpsa], [rsa])
                            psb_, rpsb = PS()
                            for kc in range(8):
                                MM(psb_[:], vm[:, kc, 128:256], uT[:, kc, cols], kc == 0, kc == 7, [rwm, R_uT[tb]], [rpsb])
                            sb_, rsb = T32()
                            ACT(sb_[:], psb_[:], AF.Sigmoid, [rpsb], [rsb])
                            pa, rpa = PS()
                            for h in range(8):
                                MM(pa[:], vm[0:64, h, 256:384], attblk[0:64, h, :], h == 0, h == 7, [rwm, R_A3], [rpa])
                            pr, rpr = PS()
                            for h in range(8):
                                MM(pr[:], vm[:, h, 384:512], retblk[:, h, :], h == 0, h == 7, [rwm, R_A4], [rpr])
                            TT(sa[:], sa[:], pa[:], ALU.mult, [rpa], [rsa])
                            TT(sb_[:], sb_[:], pr[:], ALU.mult, [rpr], [rsb])
                            TT(mT[:, fc, :], sa[:], sb_[:], ALU.add, [rsa, rsb], [R_A2])
                    for half in range(2):
                        wo, rwo = WNEXT()
                        vo = wo[:, :].rearrange("p (k f) -> p k f", f=512)
                        for f4 in range(4):
                            fc = half * 4 + f4
                            ps, rps = PS()
                            for kc in range(8):
                                MM(ps[:], vo[:, kc, f4 * 128:(f4 + 1) * 128], mT[:, kc, :], kc == 0, kc == 7, [rwo, R_A2], [rps])
                            ACT(zblk[:, fc, :], ps[:], AF.Copy, [rps], [R_A8])
                    postnorm_add(zfn, [R_A8], l, 1, hfn, [R_A7])
                    prenorm(hfn, [R_A7], l, 2, lambda kc: u2T[:, kc, :], [R_A1])
                    for half in range(2):
                        for q4 in range(4):
                            wu, rwu = WNEXT()
                            vu = wu[:, :].rearrange("p (k f) -> p k f", f=512)
                            for f4 in range(4):
                                hc = q4 * 4 + f4
                                ps, rps = PS()
                                for kc in range(8):
                                    MM(ps[:], vu[:, kc, f4 * 128:(f4 + 1) * 128], u2T[:, kc, :], kc == 0, kc == 7,
                                       [rwu, R_A1], [rps])
                                rl, rrl = T32()
                                ACT(rl[:], ps[:], AF.Relu, [rps], [rrl])
                                TT(hid[hc // 8][:, hc % 8, :], rl[:], rl[:], ALU.mult, [rrl], [R_hid[hc // 8]])
                        for q4 in range(4):
                            wd, rwd = WNEXT()
                            vd = wd[:, :].rearrange("p (k f) -> p k f", f=256)
                            for f2 in range(2):
                                fc = q4 * 2 + f2
                                ps, rps = PS()
                                for hc in range(16):
                                    MM(ps[:], vd[:, hc, f2 * 128:(f2 + 1) * 128], hid[hc // 8][:, hc % 8, :], hc == 0, hc == 15,
                                       [rwd, R_hid[hc // 8]], [rps])
                                if half == 0:
                                    ACT(zblk[:, fc, :], ps[:], AF.Copy, [rps], [R_A8])
                                else:
                                    TT(zblk[:, fc, :], ps[:], zblk[:, fc, :], ALU.add, [rps], [R_A8])
                    postnorm_add(zfn, [R_A8], l, 3, hfn, [R_A7])
                    prenorm(hfn, [R_A7], l, None, lambda kc: u2T[:, kc, :], [R_A1])
                    pf, rpf = T32(); pf2, rpf2 = T32()
                    for i2, (pt_, rp_) in enumerate(((pf, rpf), (pf2, rpf2))):
                        DMA("sp", pt_[:].rearrange("p (t f) -> p t f", f=256),
                            p_d[l][tb * TB + i2 * 256:tb * TB + (i2 + 1) * 256, :].rearrange("(t p) f -> p t f", p=128),
                            C_misc[5], [], [rp_])
                    pTt = []
                    for fk in range(2):
                        ps, rps = PS()
                        for tt in range(4):
                            src_t, rsrc = ((pf, rpf), (pf2, rpf2))[tt // 2]
                            TR(ps[:, tt * 128:(tt + 1) * 128],
                               src_t[:].rearrange("p (t f) -> p t f", f=256)[:, tt % 2, fk * 128:(fk + 1) * 128], ident32,
                               [rsrc, R_misc], [rps])
                        pT_, rpT = T16()
                        ACT(pT_[:], ps[:], AF.Copy, [rps], [rpT])
                        pTt.append((pT_, rpT))
                    wpp, rwpp = WNEXT()
                    vpp = wpp[:, 0:2048].rearrange("p (k f) -> p k f", f=1024)
                    for fc in range(8):
                        ps, rps = PS()
                        for kc in range(2):
                            MM(ps[:], vpp[:, kc, fc * 128:(fc + 1) * 128], pTt[kc][0][:], kc == 0, kc == 1,
                               [rwpp, pTt[kc][1]], [rps])
                        ACT(zblk[:, fc, :], ps[:], AF.Copy, [rps], [R_A8])
                    rs, rrs = rms_rstd(zfn, [R_A8])
                    for half in range(2):
                        wpg, rwpg = WNEXT()
                        vpg = wpg[:, :].rearrange("p (k f) -> p k f", f=512)
                        for f4 in range(4):
                            fc = half * 4 + f4
                            ps, rps = PS()
                            for kc in range(8):
                                MM(ps[:], vpg[:, kc, f4 * 128:(f4 + 1) * 128], u2T[:, kc, :], kc == 0, kc == 7, [rwpg, R_A1], [rps])
                            gsg, rgsg = T32()
                            ACT(gsg[:], ps[:], AF.Sigmoid, [rps], [rgsg])
                            tmp, rt = T32()
                            STT(tmp[:], zblk[:, fc, :], gcol(l, 4, fc), rs[:], ALU.mult, ALU.mult, [R_A8, rrs, R_const], [rt])
                            TT(tmp[:], tmp[:], gsg[:], ALU.mult, [rgsg], [rt])
                            TT(hblk[:, fc, :], hblk[:, fc, :], tmp[:], ALU.add, [rt], [R_A7])
                    if not last:
                        DMA("sp", hT_v[:, :, cols], hblk, C_h[1], [R_A7], [R_hT[tb]])
                        prenorm(hfn, [R_A7], l + 1, 0, lambda kc: uT[:, kc, cols], [R_uT[tb]])
                    else:
                        oblk = A8[:, :].rearrange("p (t f) -> p t f", f=1024)
                        for tt in range(4):
                            for k2 in range(2):
                                ps, rps = PS()
                                for kq in range(4):
                                    kc = k2 * 4 + kq
                                    TR(ps[:, kq * 128:(kq + 1) * 128], hblk[:, kc, tt * 128:(tt + 1) * 128], ident32,
                                       [R_A7, R_misc], [rps])
                                ACT(oblk[:, tt, k2 * 512:(k2 + 1) * 512], ps[:], AF.Copy, [rps], [R_A8])
                        it = DMA("sp", out_d[cols, :].rearrange("(t p) f -> p t f", p=128), oblk, C_out[tb % 4], [R_A8], [])
                        out_dmas.append(it)

        except _Stop:
            pass

        fence = S.add("sp", lambda e: e.nop(), reads=[], writes=[])
        for ch in S.chans:
            if ch.last is not None and ch.last not in fence.deps:
                fence.deps.append(ch.last)

        nep = S.finalize()
        esems = {e: [es.enter_context(nc.semaphore(f"{e}{k}")) for k in range(nep[e])] for e in S.ENGS}
        csems = [es.enter_context(nc.semaphore(f"c{i}")) for i in range(len(S.chans))]
        with nc.Block() as block:
            S.emit(nc, block, esems, csems)
        nc._sched = S
    return nc


FUSED = True
_PROG = {}


def _get_prog(NL):
    if NL not in _PROG:
        _PROG[NL] = build_program(NL)
    return _PROG[NL]


def _gains_layout(gl):
    NL = gl.shape[0]
    return np.ascontiguousarray(gl.reshape(NL, 5, 8, 128).transpose(3, 0, 1, 2).reshape(128, NL * 40))


def kernel(x, p, w_in, w_att_out, w_ret_out, w_out, w_mlp_up, w_mlp_down, w_ple_gate, w_ple_proj,
           ret_decay_logit, norm_mix_pre, norm_mix_post, norm_mlp_pre, norm_mlp_post, norm_ple):
    f = lambda a: np.ascontiguousarray(np.asarray(a, dtype=np.float32))
    x = f(x); p = f(p)
    W = dict(w_in=f(w_in), w_att_out=f(w_att_out), w_ret_out=f(w_ret_out), w_out=f(w_out),
             w_mlp_up=f(w_mlp_up), w_mlp_down=f(w_mlp_down), w_ple_gate=f(w_ple_gate), w_ple_proj=f(w_ple_proj))
    dec = f(ret_decay_logit).reshape(-1, 16)
    gl = np.stack([f(norm_mix_pre), f(norm_mix_post), f(norm_mlp_pre), f(norm_mlp_post), f(norm_ple)], axis=1)
    rope, misc = host_consts()
    B = x.shape[0]
    depth = W["w_in"].shape[0]
    groups = [list(range(depth))] if FUSED else [[l] for l in range(depth)]
    h = x
    for ls in groups:
        nc = _get_prog(len(ls))
        in_maps = []
        for b in range(B):
            m = {k: np.ascontiguousarray(v[ls]) for k, v in W.items()}
            m["x"] = np.ascontiguousarray(h[b])
            m["p"] = np.ascontiguousarray(p[ls, b])
            m["dec"] = np.ascontiguousarray(dec[ls])
            m["gains"] = _gains_layout(gl[ls])
            m["rope"] = rope
            m["misc"] = misc
            in_maps.append(m)
        res = run_bass_kernel_spmd(nc, in_maps, core_ids=list(range(B)))
        h = np.stack([np.asarray(r["out"], dtype=np.float32) for r in res.results], axis=0)
    return h
```
